# Optimizing a Trainium2 kernel written in Bass

```python
import jax, jax.numpy as jnp
from jax import lax
import numpy as np

D_MODEL = 1024
BATCH = 8
SEQ = 4096
DEPTH = 2
DEC_BATCH = 16
DEC_SEQ = 32
PAST_LEN = 2048

CHUNK = 64
N_HEADS = 8
N_KV_HEADS = 2
HEAD_DIM = 64
ATTN_DIM = N_HEADS * HEAD_DIM
KV_DIM = N_KV_HEADS * HEAD_DIM
IDX_HEADS = 4
IDX_DIM = 64
TOPK_MAX = 256
Q_BLOCK = 128
GMLP_CHUNK = 128
GMLP_GROUPS = 4
GMLP_DIM = 512
GMLP_GROUP_DIM = GMLP_DIM // GMLP_GROUPS
D_FF = 4 * D_MODEL
ROPE_THETA = 500000.0
ROPE_DIM = HEAD_DIM // 4
IDX_ROPE_DIM = IDX_DIM // 4
ALPHA = (2 * DEPTH) ** 0.25
BETA = (8 * DEPTH) ** -0.25
LN_EPS = 1e-5
IN_DIM = ATTN_DIM + 2 * KV_DIM + IDX_HEADS * IDX_DIM + IDX_DIM + IDX_HEADS + 2 * GMLP_DIM + 2 * D_MODEL

kernel_name = 'dsa_gmlp_gated_hybrid_stream_step'


def _layer_norm(x, g, b):
    xf = x.astype(jnp.float32)
    mu = jnp.mean(xf, axis=-1, keepdims=True)
    var = jnp.mean(jnp.square(xf - mu), axis=-1, keepdims=True)
    y = (xf - mu) * lax.rsqrt(var + LN_EPS)
    return (y * g.astype(jnp.float32) + b.astype(jnp.float32)).astype(x.dtype)


def _rope(x, pos, rot_dim):
    half = rot_dim // 2
    freqs = ROPE_THETA ** (-jnp.arange(half, dtype=jnp.float32) * 2.0 / rot_dim)
    ang = pos.astype(jnp.float32)[:, None] * freqs[None, :]
    ang = ang.reshape(ang.shape[:1] + (1,) * (x.ndim - 3) + (half,))
    cos, sin = jnp.cos(ang), jnp.sin(ang)
    xf = x.astype(jnp.float32)
    x1, x2, rest = xf[..., :half], xf[..., half:rot_dim], xf[..., rot_dim:]
    out = jnp.concatenate([x1 * cos - x2 * sin, x1 * sin + x2 * cos, rest], axis=-1)
    return out.astype(x.dtype)


def _mixer_inputs(x, pos, w_in, idx_k_g, idx_k_b):
    B, T, _ = x.shape
    sizes = (ATTN_DIM, KV_DIM, KV_DIM, IDX_HEADS * IDX_DIM, IDX_DIM, IDX_HEADS,
             GMLP_DIM, GMLP_DIM, D_MODEL, D_MODEL)
    points = [int(p) for p in np.cumsum(sizes)[:-1]]
    q, k, v, qi, ki, wi, u, vg, ga, gb = jnp.split(x @ w_in, points, axis=-1)
    q = _rope(q.reshape(B, T, N_HEADS, HEAD_DIM), pos, ROPE_DIM)
    k = _rope(k.reshape(B, T, N_KV_HEADS, HEAD_DIM), pos, ROPE_DIM)
    v = v.reshape(B, T, N_KV_HEADS, HEAD_DIM)
    qi = _rope(qi.reshape(B, T, IDX_HEADS, IDX_DIM), pos, IDX_ROPE_DIM)
    ki = _rope(_layer_norm(ki, idx_k_g, idx_k_b), pos, IDX_ROPE_DIM)
    return q, k, v, qi, ki, wi, u, vg, ga, gb


def _sparse_attend(q, qi, wi, qpos, k, v, ki, topk):
    B, Q = q.shape[:2]
    S = k.shape[1]
    limit = (qpos // CHUNK + 1) * CHUNK
    dots = jnp.einsum('bqhd,bsd->bqhs', qi.astype(jnp.float32), ki.astype(jnp.float32)) * IDX_DIM ** -0.5
    score = jnp.einsum('bqh,bqhs->bqs', wi.astype(jnp.float32) * IDX_HEADS ** -0.5, jax.nn.relu(dots))
    admissible = jnp.arange(S)[None, :] < limit[:, None]
    score = jnp.where(admissible[None], score, -jnp.inf)
    _, sel = lax.top_k(score, topk)
    valid = sel < limit[None, :, None]
    k_sel = jax.vmap(lambda kk, ii: kk[ii])(k, sel)
    v_sel = jax.vmap(lambda vv, ii: vv[ii])(v, sel)
    qg = q.reshape(B, Q, N_KV_HEADS, N_HEADS // N_KV_HEADS, HEAD_DIM)
    logits = jnp.einsum('bqngd,bqknd->bqngk', qg.astype(jnp.float32), k_sel.astype(jnp.float32)) * HEAD_DIM ** -0.5
    logits = jnp.where(valid[:, :, None, None, :], logits, -jnp.inf)
    p = jax.nn.softmax(logits, axis=-1)
    o = jnp.einsum('bqngk,bqknd->bqngd', p.astype(v_sel.dtype), v_sel)
    return o.reshape(B, Q, ATTN_DIM)


def _prompt_attention(q, k, v, qi, ki, wi, topk):
    B, T = q.shape[:2]
    nb = T // Q_BLOCK

    def blockify(a):
        return jnp.moveaxis(a.reshape((B, nb, Q_BLOCK) + a.shape[2:]), 1, 0)

    def one_block(args):
        qb, qib, wib, blk = args
        qpos = blk * Q_BLOCK + jnp.arange(Q_BLOCK)
        return _sparse_attend(qb, qib, wib, qpos, k, v, ki, topk)

    out = lax.map(one_block, (blockify(q), blockify(qi), blockify(wi), jnp.arange(nb)))
    return jnp.moveaxis(out, 0, 1).reshape(B, T, ATTN_DIM)


def _causal_ws(w_s):
    mask = jnp.tril(jnp.ones((GMLP_CHUNK, GMLP_CHUNK), dtype=bool))
    return jnp.where(mask[None], w_s, jnp.zeros_like(w_s))


def _sgu_prompt(u, vn, w_s, b_s):
    B, T, _ = u.shape
    vr = vn.reshape(B, T // GMLP_CHUNK, GMLP_CHUNK, GMLP_GROUPS, GMLP_GROUP_DIM)
    mix = jnp.einsum('gts,bcsgd->bctgd', _causal_ws(w_s), vr) + b_s.T[:, :, None]
    return u * mix.reshape(B, T, GMLP_DIM)


def _sgu_sample(u, vn, w_s, b_s):
    B, T, _ = u.shape
    vr = vn.reshape(B, T, GMLP_GROUPS, GMLP_GROUP_DIM)
    mix = jnp.einsum('gts,bsgd->btgd', _causal_ws(w_s)[:, :T, :T], vr) + b_s[:, :T].T[:, :, None]
    return u * mix.reshape(B, T, GMLP_DIM)


def _post_block(x, o_a, o_b, ga, gb, w_pa, w_pb, w_out, ln1_g, ln1_b, w_ff1, w_ff2, ln2_g, ln2_b):
    merged = jax.nn.sigmoid(ga) * (o_a @ w_pa) + jax.nn.sigmoid(gb) * (o_b @ w_pb)
    x = _layer_norm(ALPHA * x + merged @ w_out, ln1_g, ln1_b)
    ff = jnp.square(jax.nn.relu(x @ w_ff1)) @ w_ff2
    return _layer_norm(ALPHA * x + ff, ln2_g, ln2_b)


def setup_inputs(seed: int = 0) -> dict:
    key = jax.random.key(seed)
    ks = jax.random.split(key, 24)

    def nrm(k, shape, scale):
        return jax.random.normal(k, shape, jnp.float32) * scale

    return {
        'x_prompt': nrm(ks[0], (BATCH, SEQ, D_MODEL), 1.0),
        'x_sample': nrm(ks[1], (DEC_BATCH, DEC_SEQ, D_MODEL), 1.0),
        'cache_k': nrm(ks[2], (DEPTH, DEC_BATCH, PAST_LEN, N_KV_HEADS, HEAD_DIM), 1.0),
        'cache_v': nrm(ks[3], (DEPTH, DEC_BATCH, PAST_LEN, N_KV_HEADS, HEAD_DIM), 1.0),
        'cache_idx_k': nrm(ks[4], (DEPTH, DEC_BATCH, PAST_LEN, IDX_DIM), 1.0),
        'w_in': nrm(ks[5], (DEPTH, D_MODEL, IN_DIM), D_MODEL ** -0.5),
        'idx_k_g': 1.0 + nrm(ks[6], (DEPTH, IDX_DIM), 0.02),
        'idx_k_b': nrm(ks[7], (DEPTH, IDX_DIM), 0.02),
        'sgu_ln_g': 1.0 + nrm(ks[8], (DEPTH, GMLP_DIM), 0.02),
        'sgu_ln_b': nrm(ks[9], (DEPTH, GMLP_DIM), 0.02),
        'w_s': nrm(ks[10], (DEPTH, GMLP_GROUPS, GMLP_CHUNK, GMLP_CHUNK), 0.5 * GMLP_CHUNK ** -0.5),
        'b_s': 1.0 + nrm(ks[11], (DEPTH, GMLP_GROUPS, GMLP_CHUNK), 0.01),
        'w_pa': nrm(ks[12], (DEPTH, ATTN_DIM, D_MODEL), ATTN_DIM ** -0.5),
        'w_pb': nrm(ks[13], (DEPTH, GMLP_DIM, D_MODEL), GMLP_DIM ** -0.5),
        'w_out': nrm(ks[14], (DEPTH, D_MODEL, D_MODEL), BETA * D_MODEL ** -0.5),
        'ln1_g': 1.0 + nrm(ks[15], (DEPTH, D_MODEL), 0.02),
        'ln1_b': nrm(ks[16], (DEPTH, D_MODEL), 0.02),
        'w_ff1': nrm(ks[17], (DEPTH, D_MODEL, D_FF), D_MODEL ** -0.5),
        'w_ff2': nrm(ks[18], (DEPTH, D_FF, D_MODEL), BETA * D_FF ** -0.5),
        'ln2_g': 1.0 + nrm(ks[19], (DEPTH, D_MODEL), 0.02),
        'ln2_b': nrm(ks[20], (DEPTH, D_MODEL), 0.02),
    }


def reference(x_prompt, x_sample, cache_k, cache_v, cache_idx_k, w_in, idx_k_g, idx_k_b,
              sgu_ln_g, sgu_ln_b, w_s, b_s, w_pa, w_pb, w_out, ln1_g, ln1_b,
              w_ff1, w_ff2, ln2_g, ln2_b):
    t_prompt = x_prompt.shape[1]
    t_sample = x_sample.shape[1]
    past = cache_k.shape[2]
    pos_p = jnp.arange(t_prompt)
    pos_s = past + jnp.arange(t_sample)
    topk_p = min(TOPK_MAX, t_prompt // 4)
    topk_s = min(TOPK_MAX, (past + t_sample) // 4)

    xp, xs = x_prompt, x_sample
    pk, pv, pik, sk, sv, sik, ssv = [], [], [], [], [], [], []
    for l in range(DEPTH):
        q, k, v, qi, ki, wi, u, vg, ga, gb = _mixer_inputs(xp, pos_p, w_in[l], idx_k_g[l], idx_k_b[l])
        o_a = _prompt_attention(q, k, v, qi, ki, wi, topk_p)
        o_b = _sgu_prompt(u, _layer_norm(vg, sgu_ln_g[l], sgu_ln_b[l]), w_s[l], b_s[l])
        xp = _post_block(xp, o_a, o_b, ga, gb, w_pa[l], w_pb[l], w_out[l], ln1_g[l], ln1_b[l],
                         w_ff1[l], w_ff2[l], ln2_g[l], ln2_b[l])
        pk.append(k)
        pv.append(v)
        pik.append(ki)

        q, k, v, qi, ki, wi, u, vg, ga, gb = _mixer_inputs(xs, pos_s, w_in[l], idx_k_g[l], idx_k_b[l])
        k_all = jnp.concatenate([cache_k[l], k], axis=1)
        v_all = jnp.concatenate([cache_v[l], v], axis=1)
        ki_all = jnp.concatenate([cache_idx_k[l], ki], axis=1)
        o_a = _sparse_attend(q, qi, wi, pos_s, k_all, v_all, ki_all, topk_s)
        vn = _layer_norm(vg, sgu_ln_g[l], sgu_ln_b[l])
        o_b = _sgu_sample(u, vn, w_s[l], b_s[l])
        xs = _post_block(xs, o_a, o_b, ga, gb, w_pa[l], w_pb[l], w_out[l], ln1_g[l], ln1_b[l],
                         w_ff1[l], w_ff2[l], ln2_g[l], ln2_b[l])
        sk.append(k)
        sv.append(v)
        sik.append(ki)
        ssv.append(vn)

    return (xp, xs, jnp.stack(pk), jnp.stack(pv), jnp.stack(pik),
            jnp.stack(sk), jnp.stack(sv), jnp.stack(sik), jnp.stack(ssv))
```

```python
from contextlib import ExitStack
import numpy as np
import ml_dtypes
import concourse.bass as bass
import concourse.mybir as mybir
from concourse.bass_utils import run_bass_kernel_spmd

F32 = mybir.dt.float32
BF16 = mybir.dt.bfloat16
ALU = mybir.AluOpType
AF = mybir.ActivationFunctionType
AX = mybir.AxisListType

D = 1024
NL = 2
DFF = 4096
IN_DIM = 4164
TOPK = 256
NIT = 18
ALPHA = (2 * NL) ** 0.25
EPS = 1e-5
NEG = -1.0e30
EP = 30000
MULT_ENG = "dve"
SPLIT = 0.45


class Trk:
    __slots__ = ("w", "r")

    def __init__(self):
        self.w = None
        self.r = []


class Tile:
    def __init__(self, t, trks=None, excl=False):
        self.t = t
        self.trks = trks if trks is not None else [Trk()]
        self.excl = excl

    def __getitem__(self, k):
        return self.t[k]


class DmaSem:
    def __init__(self, h, idx):
        self.h = h
        self.idx = idx
        self.val = 0


class Eng:
    def __init__(self, name, sems, selfdep=True):
        self.name = name
        self.sems = sems
        self.n = 0
        self.waited = {}
        self.items = []
        self.selfdep = selfdep
        self.pending = False


class Sched:
    def __init__(self, nc, es):
        self.nc = nc
        self.es = es
        self.nsem = 0

        def mk(n):
            out = []
            for _ in range(n):
                out.append(es.enter_context(nc.semaphore(f"s{self.nsem}")))
                self.nsem += 1
            return out

        self.E = {
            "pe": Eng("pe", mk(3), selfdep=False),
            "act": Eng("act", mk(3)),
            "dve": Eng("dve", mk(3)),
            "pool": Eng("pool", mk(2)),
            "sp": Eng("sp", []),
        }
        self.dsems = []
        self.hw = {"pe": nc.tensor, "act": nc.scalar, "dve": nc.vector, "pool": nc.gpsimd, "sp": nc.sync}
        self.ninst = {}

    def dma_sem(self):
        h = self.es.enter_context(self.nc.semaphore(f"d{self.nsem}"))
        self.nsem += 1
        s = DmaSem(h, len(self.dsems))
        self.dsems.append(s)
        return s

    def _waits(self, eng, r, w):
        toks = []
        for tl in r:
            for k in tl.trks:
                if k.w is not None:
                    toks.append(k.w)
                if tl.excl:
                    toks.extend(tk for tk in k.r if not (tk[0] == "e" and tk[1] == eng))
        for tl in w:
            for k in tl.trks:
                if k.w is not None:
                    toks.append(k.w)
                toks.extend(k.r)
        e = self.E[eng]
        best = {}
        for tok in toks:
            if tok[0] == "e":
                _, en, n = tok
                if en == eng and not e.selfdep:
                    continue
                if n > self.E[en].n:
                    raise RuntimeError(f"token of {en} not signalled yet ({n}>{self.E[en].n})")
                key = en
                val = n
            else:
                _, ds, val = tok
                key = ("d", ds.idx)
            if val > best.get(key, 0):
                best[key] = val
        waits = []
        for key, val in best.items():
            if e.waited.get(key, 0) >= val:
                continue
            e.waited[key] = val
            if isinstance(key, str):
                waits.append((self.E[key].sems[(val - 1) // EP], (val - 1) % EP + 1))
            else:
                waits.append((self.dsems[key[1]].h, val))
        return waits

    def op(self, eng, fn, r=(), w=(), signal=True):
        e = self.E[eng]
        waits = self._waits(eng, r, w)
        if signal:
            e.n += 1
            n = e.n
            inc = (e.sems[(n - 1) // EP], 1)
            tok = ("e", eng, n)
            e.pending = False
        else:
            inc = None
            tok = ("e", eng, e.n + 1)
            e.pending = True
        self._emit(eng, waits, fn, inc)
        for tl in r:
            for k in tl.trks:
                k.r.append(tok)
        for tl in w:
            for k in tl.trks:
                k.w = tok
                k.r = []
        return tok

    def _emit(self, eng, waits, fn, inc):
        en = self.hw[eng]
        for (h, v) in waits:
            en.wait_ge(h, v)
        if fn is None:
            return
        ins = fn(en)
        if inc is not None:
            ins.then_inc(inc[0], inc[1])
        self.ninst[eng] = self.ninst.get(eng, 0) + 1 + len(waits)

    def dma(self, q, out, in_, sem, r=(), w=(), **kw):
        e = self.E[q]
        waits = self._waits(q, r, w)
        sem.val += 16
        tok = ("d", sem, sem.val)
        self._emit(q, waits, lambda en: en.dma_start(out=out, in_=in_, **kw), (sem.h, 16))
        for tl in r:
            for k in tl.trks:
                k.r.append(tok)
        for tl in w:
            for k in tl.trks:
                k.w = tok
                k.r = []
        return tok

    def final_waits(self, q):
        e = self.E[q]
        waits = []
        for ds in self.dsems:
            if ds.val > 0 and e.waited.get(("d", ds.idx), 0) < ds.val:
                waits.append((ds.h, ds.val))
        for en, src in self.E.items():
            if en == q or not src.sems or src.n == 0:
                continue
            n = src.n
            waits.append((src.sems[(n - 1) // EP], (n - 1) % EP + 1))
        self._emit(q, waits, None, None)

    def replay(self):
        nc = self.nc
        engmap = {"pe": "tensor", "act": "scalar", "dve": "vector", "pool": "gpsimd", "sp": "sync"}
        with nc.Block() as block:
            for name, e in self.E.items():
                items = e.items

                def body(en, items=items):
                    for waits, fn, inc in items:
                        for (h, v) in waits:
                            en.wait_ge(h, v)
                        if fn is None:
                            continue
                        ins = fn(en)
                        if inc is not None:
                            ins.then_inc(inc[0], inc[1])

                getattr(block, engmap[name])(body)


class _Stop(Exception):
    pass


def build(SEQ, PAST, stop=0):
    nc = bass.Bass("TRN2", target_bir_lowering=False)
    es = ExitStack()
    S = Sched(nc, es)
    NTT = SEQ // 128
    NST = SEQ // 512
    SMAX = max(SEQ, ((PAST + 32 + 127) // 128) * 128)
    NBLK = SMAX // 128
    PBLK = PAST // 128

    def din(name, shape, dt=F32):
        return nc.dram_tensor(name, list(shape), dt, kind="ExternalInput").ap()

    def dout(name, shape, dt=F32):
        return nc.dram_tensor(name, list(shape), dt, kind="ExternalOutput").ap()

    xp = din("xp", [SEQ, D])
    xs = din("xs", [64, D])
    ck = din("ck", [NL, 2, PAST, 128])
    cv = din("cv", [NL, 2, PAST, 128])
    cik = din("cik", [NL, 2, PAST, 64])
    WSH = {"w_in": [NL, D, IN_DIM], "w_pa": [NL, 512, D], "w_pb": [NL, 512, D], "w_out": [NL, D, D],
           "w_ff1": [NL, D, DFF], "w_ff2": [NL, DFF, D]}
    wf = {k: din(k, v) for k, v in WSH.items()}
    wb = {k: nc.dram_tensor(k + "_b", list(v), BF16, kind="Internal").ap() for k, v in WSH.items()}
    idx_g = din("idx_k_g", [NL, 64]); idx_b = din("idx_k_b", [NL, 64])
    sgu_g = din("sgu_ln_g", [NL, 512]); sgu_b = din("sgu_ln_b", [NL, 512])
    w_s = din("w_s", [NL, 4, 128, 128]); b_s = din("b_s", [NL, 4, 128])
    ln1g = din("ln1_g", [NL, D]); ln1b = din("ln1_b", [NL, D])
    ln2g = din("ln2_g", [NL, D]); ln2b = din("ln2_b", [NL, D])
    c_ident = din("c_ident", [128, 128], BF16)
    c_cmask = din("c_cmask", [128, 128], BF16)
    c_rope = din("c_rope", [128, NTT + 1, 32])
    c_steps = din("c_steps", [128, NIT + 2])

    yp = dout("yp", [SEQ, D]); ys = dout("ys", [64, D])
    pk = dout("pk", [NL, SEQ, 128]); pv = dout("pv", [NL, SEQ, 128]); pik = dout("pik", [NL, SEQ, 64])
    sk = dout("sk", [NL, 64, 128]); sv = dout("sv", [NL, 64, 128]); sik = dout("sik", [NL, 64, 64])
    ssv = dout("ssv", [NL, 64, 512])

    def sb(name, shape, dt, trks=None):
        return Tile(es.enter_context(nc.sbuf_tensor(name, list(shape), dt)), trks)

    ident = sb("ident", [128, 128], BF16)
    cmask = sb("cmask", [128, 128], BF16)
    rope = sb("rope", [128, NTT + 1, 32], F32)
    steps = sb("steps", [128, NIT + 2], F32)
    neghalf = sb("neghalf", [128, 1], F32)
    WsT = [sb(f"WsT{l}", [128, 4, 128], BF16) for l in range(NL)]
    bsT = [sb(f"bsT{l}", [128, 4], F32) for l in range(NL)]
    KT = [sb(f"KT{l}", [128, SMAX], BF16) for l in range(NL)]
    Vc = [sb(f"V{l}", [128, NBLK, 2, 65], BF16) for l in range(NL)]
    KiT = [sb(f"KiT{l}", [128, SMAX], BF16) for l in range(NL)]
    p_ln = sb("p_ln", [128, 2, D], F32)
    p_sgu = sb("p_sgu", [128, 2, 512], F32)
    p_idx = sb("p_idx", [128, 2, 64], F32)
    xres = sb("xres", [128, 4, D], F32, trks=[Trk() for _ in range(4)])
    xrt = [Tile(xres.t, [xres.trks[i]]) for i in range(4)]
    actT = sb("actT", [128, 8, 512], BF16)
    QT = sb("QT", [128, 4, 512], BF16)
    QiT = sb("QiT", [128, 4, 2, 128], BF16)
    absw = sb("absw", [128, 4, 4], F32)
    sgnw = sb("sgnw", [128, 4, 4], F32)
    ub = sb("ub", [128, 4, 512], BF16)
    ObT = sb("ObT", [128, 4, 512], BF16)
    OaT = sb("OaT", [128, 4, 512], BF16)
    gates = sb("gates", [128, 4, 2048], BF16)
    big = es.enter_context(nc.sbuf_tensor("big", [128, 8192], F32))
    t_sc, t_mk, t_mt = Trk(), Trk(), Trk()
    scores = Tile(big[:, 0:4096], [t_sc])
    bigb = big[:].bitcast(BF16)
    maskb = Tile(bigb[:, 8192:12288], [t_mk])
    maskT = Tile(bigb[:, 12288:16384], [t_mt])
    maskb_hi = Tile(bigb[:, 8192:12288], [Trk()])
    HT = Tile(bigb, [t_sc, t_mk, t_mt])
    NW = 5
    wring = [sb(f"wr{i}", [128, 4, 512], BF16) for i in range(NW)]
    wsem = [S.dma_sem() for _ in range(NW)]
    wcnt = [0]
    PTr = [sb(f"PT{i}", [128, 512], BF16) for i in range(4)]
    ptc = [0]
    f32r = [sb(f"f32r{i}", [128, 512], F32) for i in range(4)]
    f32c = [0]
    b16r = [sb(f"b16r{i}", [128, 1024], BF16) for i in range(4)]
    b16c = [0]
    smr = [sb(f"smr{i}", [128, 128], F32) for i in range(12)]
    smc = [0]
    kout = [sb(f"kout{i}", [128, 128], F32) for i in range(2)]
    vout = [sb(f"vout{i}", [128, 128], F32) for i in range(2)]
    kiout = [sb(f"kiout{i}", [128, 64], F32) for i in range(2)]
    osem = {k: S.dma_sem() for k in ["k0", "k1", "v0", "v1", "ki0", "ki1", "y"]}
    vnsem = [S.dma_sem() for _ in range(4)]
    oc = [0]
    oc2 = [0]
    tcur = sb("tcur", [128, 2], F32)
    stp = sb("stp", [128, NIT + 2], F32)
    cntt = sb("cntt", [128, 2], F32)
    cnta = sb("cnta", [128, 2], F32)
    bmax = sb("bmax", [128, 8], F32)
    dcur = sb("dcur", [128, 2], F32)
    ntc = sb("ntc", [128, 2], F32)
    nstp = sb("nstp", [128, NIT + 2], F32)

    def ring(lst, c):
        t = lst[c[0] % len(lst)]
        c[0] += 1
        return t

    banks = [Tile(es.enter_context(nc.psum_tensor(f"ps{i}", [128, 512], F32)), excl=True) for i in range(8)]
    bcnt = {k: [0] for k in ["a", "b", "all", "ip", "s"]}
    bpool = {"a": [0, 1, 2], "b": [0, 1, 2, 3, 4, 5], "all": [0, 1, 2, 3, 4, 5, 6, 7], "ip": [4, 5, 6, 7], "s": [2, 3]}

    def bank(pool="a"):
        lst = bpool[pool]
        i = lst[bcnt[pool][0] % len(lst)]
        bcnt[pool][0] += 1
        return banks[i]

    def bf(bk):
        return bk.t[:].bitcast(BF16)

    psem = {(k, l): S.dma_sem() for k in WSH for l in range(NL)}
    xsem = S.dma_sem()
    stsem = S.dma_sem()
    lsem = {k: S.dma_sem() for k in ["ln", "sgu", "idx"]}

    S.dma("sp", ident[:], c_ident[:, :], S.dma_sem(), w=[ident])
    S.dma("sp", cmask[:], c_cmask[:, :], S.dma_sem(), w=[cmask])
    S.dma("sp", rope[:], c_rope[:, :, :], S.dma_sem(), w=[rope])
    S.dma("sp", steps[:], c_steps[:, :], S.dma_sem(), w=[steps])
    S.op("dve", lambda e: e.memset(neghalf[:], -0.5), w=[neghalf])
    wtile = {(k, l): Tile(None) for k in WSH for l in range(NL)}
    for l in range(NL):
        for k, shp in WSH.items():
            rows = shp[1]
            nsplit = max(1, rows // 256)
            for i in range(nsplit):
                r0 = i * rows // nsplit
                r1 = (i + 1) * rows // nsplit
                S.dma("pool", wb[k][l, r0:r1, :], wf[k][l, r0:r1, :], psem[(k, l)], w=[wtile[(k, l)]])
    for l in range(NL):
        S.op("dve", lambda e, l=l: e.memset(Vc[l][:, :, :, 64:65], 1.0), w=[Vc[l]])
        st = ring(f32r, f32c)
        S.dma("sp", st[:, 0:512].rearrange("p (g s) -> p g s", g=4), w_s[l].rearrange("g t s -> t g s"), S.dma_sem(), w=[st])
        sbt = ring(b16r, b16c)
        S.op("dve", lambda e, st=st, sbt=sbt: e.tensor_copy(out=sbt[:, 0:512], in_=st[:, 0:512]), r=[st], w=[sbt])
        bk = bank("a")
        for g in range(4):
            S.op("pe", lambda e, g=g, bk=bk, sbt=sbt: e.transpose(out=bf(bk)[:, g * 128:(g + 1) * 128], in_=sbt[:, g * 128:(g + 1) * 128], identity=ident[:]),
                 r=[sbt, ident], w=[bk], signal=(g == 3))
        S.op("dve", lambda e, l=l, bk=bk: e.tensor_tensor(out=WsT[l][:], in0=bf(bk)[:, 0:512].rearrange("p (g t) -> p g t", g=4),
                                                      in1=cmask[:].unsqueeze(1).broadcast_to([128, 4, 128]), op=ALU.mult),
             r=[bk, cmask], w=[WsT[l]])
        S.dma("sp", bsT[l][:], b_s[l].rearrange("g t -> t g"), S.dma_sem(), w=[bsT[l]], allow_slow_non_contiguous=True)

    def stage(k):
        if stop and k >= stop:
            raise _Stop()

    def wchunk(name, l, r, c0, ncols):
        i = wcnt[0] % NW
        wcnt[0] += 1
        slot = wring[i]
        src = wb[name][l, 512 * r:512 * r + 512, c0:c0 + ncols].rearrange("(kc p) n -> p kc n", p=128)
        S.dma("sp", slot[:, :, 0:ncols], src, wsem[i], r=[wtile[(name, l)]], w=[slot])
        return slot

    def run(g):
        for _ in g:
            pass

    def rr(gens):
        gens = list(gens)
        while gens:
            for g in list(gens):
                try:
                    next(g)
                except StopIteration:
                    gens.remove(g)

    def transposes_g(src_tile, src_ap_fn, nblk, nq, dst_fn, dst_tiles, ncols=128, evac="dve", pool="a"):
        j = 0
        while j < nblk:
            nj = min(8, nblk - j)
            bk = bank(pool)
            for jj in range(nj):
                S.op("pe", lambda e, bk=bk, jj=jj, j=j: e.transpose(out=bf(bk)[0:ncols, jj * 128:jj * 128 + nq], in_=src_ap_fn(j + jj), identity=ident[0:nq, 0:nq]),
                     r=[src_tile, ident], w=[bk], signal=(jj == nj - 1))
            yield
            S.op(evac, lambda e, bk=bk, j=j, nj=nj: dst_fn(e, bf(bk), j, nj), r=[bk], w=dst_tiles)
            yield
            j += nj

    def transposes(*a, **k):
        run(transposes_g(*a, **k))

    def rope_g(buf, c0, H, nq, tix):
        x = buf[0:nq, c0:c0 + 64 * H].rearrange("p (h d) -> p h d", h=H)
        xr = x[:, :, 0:16]
        C = rope[0:nq, tix, 0:16].unsqueeze(1).broadcast_to([nq, H, 16])
        Sn = rope[0:nq, tix, 16:32].unsqueeze(1).broadcast_to([nq, H, 16])
        t1 = ring(smr, smc)
        t2 = ring(smr, smc)
        t1v = t1[0:nq, 0:16 * H].rearrange("p (h d) -> p h d", h=H)
        t2v = t2[0:nq, 0:16 * H].rearrange("p (h d) -> p h d", h=H)
        S.op("dve", lambda e: e.tensor_tensor(out=t1v, in0=xr, in1=C, op=ALU.mult), r=[buf, rope], w=[t1])
        yield
        S.op("dve", lambda e: e.tensor_tensor(out=t2v[:, :, 0:8], in0=x[:, :, 8:16], in1=Sn[:, :, 0:8], op=ALU.mult), r=[buf, rope], w=[t2])
        yield
        S.op("dve", lambda e: e.tensor_tensor(out=t2v[:, :, 8:16], in0=x[:, :, 0:8], in1=Sn[:, :, 8:16], op=ALU.mult), r=[buf, rope, t2], w=[t2])
        yield
        S.op("dve", lambda e: e.tensor_tensor(out=xr, in0=t1v, in1=t2v, op=ALU.add), r=[t1, t2], w=[buf])
        yield

    def ln_g(src_ap, dst_ap, nq, Dn, g_ap, b_ap, r, w, ptiles):
        nch = (Dn + 511) // 512
        st = ring(smr, smc)
        for c in range(nch):
            cw = min(512, Dn - c * 512)
            S.op("dve", lambda e, c=c, cw=cw: e.bn_stats(out=st[0:nq, 6 * c:6 * c + 6], in_=src_ap[:, c * 512:c * 512 + cw]), r=r, w=[st])
            yield
        mv = ring(smr, smc)
        S.op("dve", lambda e: e.bn_aggr(out=mv[0:nq, 0:2], in_=st[0:nq, 0:6 * nch]), r=[st], w=[mv])
        yield
        S.op("dve", lambda e: e.tensor_scalar(out=mv[0:nq, 2:3], in0=mv[0:nq, 1:2], scalar1=EPS, scalar2=None, op0=ALU.add), r=[mv], w=[mv])
        yield
        S.op("act", lambda e: e.activation(out=mv[0:nq, 4:5], in_=mv[0:nq, 2:3], func=AF.Ln), r=[mv], w=[mv])
        yield
        S.op("act", lambda e: e.activation(out=mv[0:nq, 3:4], in_=mv[0:nq, 4:5], func=AF.Exp, scale=-0.5), r=[mv], w=[mv])
        yield
        S.op("dve", lambda e: e.tensor_scalar(out=mv[0:nq, 5:6], in0=mv[0:nq, 0:1], scalar1=mv[0:nq, 3:4], scalar2=-1.0, op0=ALU.mult, op1=ALU.mult), r=[mv], w=[mv])
        yield
        S.op("act", lambda e: e.activation(out=dst_ap, in_=src_ap, func=AF.Identity, scale=mv[0:nq, 3:4], bias=mv[0:nq, 5:6]), r=r + [mv], w=w)
        yield
        S.op("dve", lambda e: e.tensor_tensor(out=dst_ap, in0=dst_ap, in1=g_ap, op=ALU.mult), r=w + ptiles, w=w)
        yield
        S.op("dve", lambda e: e.tensor_tensor(out=dst_ap, in0=dst_ap, in1=b_ap, op=ALU.add), r=w + ptiles, w=w)
        yield

    def layer_pass(tiles, l, last, xT_ready=False):
        NT = len(tiles)
        TT = sum(t["nq"] for t in tiles)
        S.dma("sp", p_ln[:, 0, :], ln1g[l:l + 1, :].partition_broadcast(128), lsem["ln"], w=[p_ln])
        S.dma("sp", p_ln[:, 1, :], ln1b[l:l + 1, :].partition_broadcast(128), lsem["ln"], w=[p_ln])
        S.dma("sp", p_sgu[:, 0, :], sgu_g[l:l + 1, :].partition_broadcast(128), lsem["sgu"], w=[p_sgu])
        S.dma("sp", p_sgu[:, 1, :], sgu_b[l:l + 1, :].partition_broadcast(128), lsem["sgu"], w=[p_sgu])
        S.dma("sp", p_idx[:, 0, :], idx_g[l:l + 1, :].partition_broadcast(128), lsem["idx"], w=[p_idx])
        S.dma("sp", p_idx[:, 1, :], idx_b[l:l + 1, :].partition_broadcast(128), lsem["idx"], w=[p_idx])

        def make_xT_g(ti, t, pool="a"):
            nq, col = t["nq"], t["col"]
            xb = ring(b16r, b16c)
            S.op("act", lambda e: e.activation(out=xb[0:nq, :], in_=xres[0:nq, ti, :], func=AF.Copy), r=[xrt[ti]], w=[xb])
            yield
            yield from transposes_g(xb, lambda j: xb[0:nq, j * 128:(j + 1) * 128], 8, nq,
                                    lambda e, bv, j, nj: e.tensor_copy(out=actT[:, j:j + nj, col:col + nq], in_=bv[:, 0:nj * 128].rearrange("p (j q) -> p j q", j=nj)[:, :, 0:nq]),
                                    [actT], pool=pool)

        def ln_xT_g(ti, t, do_xT):
            nq = t["nq"]
            yield from ln_g(xres[0:nq, ti, :], xres[0:nq, ti, :], nq, D, p_ln[0:nq, 0, :], p_ln[0:nq, 1, :], [xrt[ti]], [xrt[ti]], [p_ln])
            if do_xT:
                yield from make_xT_g(ti, t, pool="all")

        if not xT_ready:
            for p0 in range(0, NT, 2):
                rr([make_xT_g(ti, tiles[ti]) for ti in range(p0, min(p0 + 2, NT))])

        stage(2)
        blocks = [(0, 512), (512, 512), (1024, 68), (1092, 512), (1604, 512), (2116, 512), (2628, 512), (3140, 512), (3652, 512)]

        def mm_block(bi, pool, tis=None, wc=None):
            c0, ncols = blocks[bi]
            if wc is None:
                wc = [wchunk("w_in", l, r, c0, ncols) for r in range(2)]
            accs = {}
            for ti in (range(NT) if tis is None else tis):
                t = tiles[ti]
                nq, col = t["nq"], t["col"]
                acc = bank(pool)
                for kc in range(8):
                    S.op("pe", lambda e, kc=kc: e.matmul(acc[0:nq, 0:ncols], lhsT=actT[:, kc, col:col + nq], rhs=wc[kc // 4][:, kc % 4, 0:ncols], start=(kc == 0), stop=(kc == 7)),
                         r=[actT, wc[kc // 4]], w=[acc], signal=(kc == 7 or kc == 3))
                accs[ti] = acc
                if bi == 3:
                    S.op("act", lambda e: e.activation(out=ub[0:nq, ti, :], in_=acc[0:nq, :], func=AF.Copy), r=[acc], w=[ub])
                elif bi >= 5:
                    gc = (bi - 5) * 512
                    S.op("act", lambda e: e.activation(out=gates[0:nq, ti, gc:gc + 512], in_=acc[0:nq, :], func=AF.Sigmoid), r=[acc], w=[gates])
            return accs, wc

        def chain(bi, ti, t, acc):
            nq, col, tix = t["nq"], t["col"], t["tix"]
            kT_, V_, KiT_ = t["cache"][l]
            kpos = t["kpos"]
            if bi == 0:
                qf = ring(f32r, f32c)
                S.op("act", lambda e: e.activation(out=qf[0:nq, :], in_=acc[0:nq, :], func=AF.Copy, scale=0.125), r=[acc], w=[qf])
                yield
                yield from rope_g(qf, 0, 8, nq, tix)
                qb = ring(b16r, b16c)
                S.op("dve", lambda e: e.tensor_copy(out=qb[0:nq, 0:512].rearrange("p (g n d) -> p g n d", g=4, n=2),
                                                    in_=qf[0:nq, :].rearrange("p (n g d) -> p g n d", n=2, g=4)), r=[qf], w=[qb])
                yield
                yield from transposes_g(qb, lambda j: qb[0:nq, j * 128:(j + 1) * 128], 4, nq,
                                        lambda e, bv, j, nj: e.tensor_copy(out=QT[:, ti, 0:4 * nq].rearrange("p (g q) -> p g q", g=4),
                                                                           in_=bv[:, 0:512].rearrange("p (g q) -> p g q", g=4)[:, :, 0:nq]), [QT])
            elif bi == 1:
                kf = ring(kout, oc)
                ki_ = (oc[0] - 1) % 2
                vf = vout[ki_]
                S.op("act", lambda e: e.activation(out=kf[0:nq, :], in_=acc[0:nq, 0:128], func=AF.Copy), r=[acc], w=[kf])
                yield
                S.op("act", lambda e: e.activation(out=vf[0:nq, :], in_=acc[0:nq, 128:256], func=AF.Copy), r=[acc], w=[vf])
                yield
                qif = ring(f32r, f32c)
                S.op("act", lambda e: e.activation(out=qif[0:nq, 0:256], in_=acc[0:nq, 256:512], func=AF.Copy), r=[acc], w=[qif])
                yield
                S.dma("sp", t["outs"]["v"][l], vf[0:nq, :], osem[f"v{ki_}"], r=[vf])
                S.op("dve", lambda e: e.tensor_copy(out=V_[0:nq, kpos // 128, :, 0:64], in_=vf[0:nq, :].rearrange("p (n d) -> p n d", n=2)), r=[vf], w=[V_])
                yield
                yield from rope_g(kf, 0, 2, nq, tix)
                S.dma("sp", t["outs"]["k"][l], kf[0:nq, :], osem[f"k{ki_}"], r=[kf])
                kb = ring(b16r, b16c)
                S.op("dve", lambda e: e.tensor_copy(out=kb[0:nq, 0:128], in_=kf[0:nq, :]), r=[kf], w=[kb])
                yield
                yield from transposes_g(kb, lambda j: kb[0:nq, 0:128], 1, nq,
                                        lambda e, bv, j, nj: e.tensor_copy(out=kT_[:, kpos:kpos + nq], in_=bv[:, 0:nq]), [kT_])
                yield from rope_g(qif, 0, 4, nq, tix)
                qib = ring(b16r, b16c)
                S.op("dve", lambda e: e.tensor_copy(out=qib[0:nq, 0:256], in_=qif[0:nq, 0:256]), r=[qif], w=[qib])
                yield
                yield from transposes_g(qib, lambda j: qib[0:nq, j * 128:(j + 1) * 128], 2, nq,
                                        lambda e, bv, j, nj: e.tensor_copy(out=QiT[:, ti, :, 0:nq], in_=bv[:, 0:256].rearrange("p (j q) -> p j q", j=2)[:, :, 0:nq]), [QiT])
            elif bi == 2:
                S.op("act", lambda e: e.activation(out=absw[0:nq, ti, :], in_=acc[0:nq, 64:68], func=AF.Abs, scale=0.0625), r=[acc], w=[absw])
                yield
                S.op("act", lambda e: e.activation(out=sgnw[0:nq, ti, :], in_=acc[0:nq, 64:68], func=AF.Sign), r=[acc], w=[sgnw])
                yield
                kif = ring(kiout, oc2)
                kii = (oc2[0] - 1) % 2
                yield from ln_g(acc[0:nq, 0:64], kif[0:nq, :], nq, 64, p_idx[0:nq, 0, :], p_idx[0:nq, 1, :], [acc], [kif], [p_idx])
                yield from rope_g(kif, 0, 1, nq, tix)
                S.dma("sp", t["outs"]["ki"][l], kif[0:nq, :], osem[f"ki{kii}"], r=[kif])
                kib = ring(b16r, b16c)
                S.op("dve", lambda e: e.tensor_copy(out=kib[0:nq, 0:128].rearrange("p (r d) -> p r d", r=2), in_=kif[0:nq, :].unsqueeze(1).broadcast_to([nq, 2, 64])), r=[kif], w=[kib])
                yield
                yield from transposes_g(kib, lambda j: kib[0:nq, 0:128], 1, nq,
                                        lambda e, bv, j, nj: e.tensor_copy(out=KiT_[:, kpos:kpos + nq], in_=bv[:, 0:nq]), [KiT_])
            elif bi == 4:
                vn = ring(f32r, f32c)
                yield from ln_g(acc[0:nq, :], vn[0:nq, :], nq, 512, p_sgu[0:nq, 0, :], p_sgu[0:nq, 1, :], [acc], [vn], [p_sgu])
                if t["outs"].get("vn") is not None:
                    S.dma("sp", t["outs"]["vn"][l], vn[0:nq, :], vnsem[f32r.index(vn)], r=[vn])
                vnb = ring(b16r, b16c)
                S.op("act", lambda e: e.activation(out=vnb[0:nq, 0:512], in_=vn[0:nq, :], func=AF.Copy), r=[vn], w=[vnb])
                yield
                mix = bank("a")
                for g in range(4):
                    S.op("pe", lambda e, g=g: e.matmul(mix[0:nq, g * 128:(g + 1) * 128], lhsT=WsT[l][0:nq, g, 0:nq], rhs=vnb[0:nq, g * 128:(g + 1) * 128], start=True, stop=True),
                         r=[WsT[l], vnb], w=[mix], signal=(g == 3))
                yield
                ob = ring(b16r, b16c)
                for g in range(4):
                    S.op("dve", lambda e, g=g: e.scalar_tensor_tensor(out=ob[0:nq, g * 128:(g + 1) * 128], in0=mix[0:nq, g * 128:(g + 1) * 128], scalar=bsT[l][0:nq, g:g + 1],
                                                                      in1=ub[0:nq, ti, g * 128:(g + 1) * 128], op0=ALU.add, op1=ALU.mult), r=[mix, bsT[l], ub], w=[ob])
                    yield
                yield from transposes_g(ob, lambda j: ob[0:nq, j * 128:(j + 1) * 128], 4, nq,
                                        lambda e, bv, j, nj: e.tensor_copy(out=ObT[:, 0:4, col:col + nq], in_=bv[:, 0:512].rearrange("p (j q) -> p j q", j=4)[:, :, 0:nq]), [ObT])

        halves = [list(range(0, min(2, NT))), list(range(2, NT))]
        halves = [h for h in halves if h]
        cseq, sseq = [0, 1, 2, 4], [5, 6, 7, 8]
        mm_block(3, "s")
        cur = {}
        wc0 = None
        for h in halves:
            a_, wc0 = mm_block(cseq[0], "ip", h, wc0)
            cur.update(a_)
        for i, cbi in enumerate(cseq):
            stage(2 + (cbi + 1) / 10.0)
            mm_block(sseq[i], "s")
            nxt = {}
            wcn = None
            for h in halves:
                rr([chain(cbi, ti, tiles[ti], cur[ti]) for ti in h])
                if i + 1 < len(cseq):
                    a_, wcn = mm_block(cseq[i + 1], "ip", h, wcn)
                    nxt.update(a_)
            cur = nxt

        stage(3)
        Ob = [banks[6], banks[7]]

        def SC(ti):
            t = tiles[ti]
            nq = t["nq"]
            KiT_ = t["cache"][l][2]
            Sk = t["nkeys"]
            for c0 in range(0, Sk, 512):
                cw = min(512, Sk - c0)
                for h in range(4):
                    hp, hh = h // 2, h % 2
                    dk = bank("a")
                    S.op("pe", lambda e: e.matmul(dk[0:nq, 0:cw], lhsT=QiT[64 * hh:64 * hh + 64, ti, hp, 0:nq], rhs=KiT_[64 * hh:64 * hh + 64, c0:c0 + cw], start=True, stop=True),
                         r=[QiT, KiT_], w=[dk])
                    rl = ring(f32r, f32c)
                    S.op("act", lambda e: e.activation(out=rl[0:nq, 0:cw], in_=dk[0:nq, 0:cw], func=AF.Relu, scale=absw[0:nq, ti, h:h + 1]), r=[dk, absw], w=[rl])
                    if h == 0:
                        S.op("dve", lambda e: e.tensor_scalar(out=scores[0:nq, c0:c0 + cw], in0=rl[0:nq, 0:cw], scalar1=sgnw[0:nq, ti, 0:1], scalar2=None, op0=ALU.mult), r=[rl, sgnw], w=[scores])
                    else:
                        S.op("dve", lambda e: e.scalar_tensor_tensor(out=scores[0:nq, c0:c0 + cw], in0=rl[0:nq, 0:cw], scalar=sgnw[0:nq, ti, h:h + 1], in1=scores[0:nq, c0:c0 + cw], op0=ALU.mult, op1=ALU.add),
                             r=[rl, sgnw, scores], w=[scores])
                S.op("dve", lambda e: e.tensor_reduce(out=bmax[0:nq, c0 // 512:c0 // 512 + 1], in_=scores[0:nq, c0:c0 + cw], axis=AX.X, op=ALU.max, apply_absolute_value=True), r=[scores], w=[bmax])

        def BI(ti):
            t = tiles[ti]
            nq = t["nq"]
            Sk = t["nkeys"]
            K = t["topk"]
            S1 = Sk
            if SPLIT and Sk >= 512:
                S1 = int(round(SPLIT * Sk / 64.0)) * 64
            N2 = Sk - S1
            nblk_s = (Sk + 511) // 512
            S.op("dve", lambda e: e.tensor_reduce(out=cntt[0:nq, 1:2], in_=bmax[0:nq, 0:nblk_s], axis=AX.X, op=ALU.max), r=[bmax], w=[cntt])
            S.op("dve", lambda e: e.tensor_scalar(out=cntt[0:nq, 1:2], in0=cntt[0:nq, 1:2], scalar1=1e-20, scalar2=None, op0=ALU.add), r=[cntt], w=[cntt])
            if t["causal"]:
                S.op("dve", lambda e: e.memset(scores[0:64, Sk - 64:Sk], NEG), r=[cntt], w=[scores])
            S.op("dve", lambda e: e.tensor_scalar(out=stp[0:nq, :], in0=steps[0:nq, :], scalar1=cntt[0:nq, 1:2], scalar2=None, op0=ALU.mult), r=[steps, cntt], w=[stp])
            S.op("dve", lambda e: e.tensor_scalar(out=tcur[0:nq, 0:1], in0=cntt[0:nq, 1:2], scalar1=0.37, scalar2=None, op0=ALU.mult), r=[cntt], w=[tcur])
            yield "S"
            for it in range(NIT + 1):
                S.op("dve", lambda e: e.tensor_scalar(out=maskb[0:nq, 0:S1], in0=scores[0:nq, 0:S1], scalar1=tcur[0:nq, 0:1], scalar2=0.0, op0=ALU.is_ge, op1=ALU.add, accum_out=cntt[0:nq, 0:1]),
                     r=[scores, tcur], w=[maskb, cntt])
                if N2:
                    S.op("act", lambda e: e.activation(out=maskb[0:nq, S1:Sk], in_=scores[0:nq, S1:Sk], func=AF.Sign, scale=-1.0, bias=tcur[0:nq, 0:1], accum_out=cnta[0:nq, 0:1]),
                         r=[scores, tcur], w=[maskb_hi, cnta])
                yield "A"
                if N2:
                    S.op("dve", lambda e: e.scalar_tensor_tensor(out=dcur[0:nq, 1:2], in0=cntt[0:nq, 0:1], scalar=2.0, in1=cnta[0:nq, 0:1], op0=ALU.mult, op1=ALU.subtract), r=[cntt, cnta], w=[dcur])
                    zsrc, thr = dcur[0:nq, 1:2], 2 * K - N2 - 0.5
                else:
                    zsrc, thr = cntt[0:nq, 0:1], K - 0.5
                last = (it == NIT)
                S.op("dve", lambda e: e.tensor_scalar(out=dcur[0:nq, 0:1], in0=zsrc, scalar1=thr, scalar2=(1.0 if last else 0.5), op0=ALU.is_ge, op1=ALU.subtract), r=[cntt, dcur], w=[dcur])
                S.op("dve", lambda e: e.scalar_tensor_tensor(out=tcur[0:nq, 0:1], in0=dcur[0:nq, 0:1], scalar=stp[0:nq, it + 1:it + 2], in1=tcur[0:nq, 0:1], op0=ALU.mult, op1=ALU.add), r=[tcur, dcur, stp], w=[tcur])
                yield "B"
            S.op("dve", lambda e: e.tensor_scalar(out=maskb[0:nq, 0:Sk], in0=scores[0:nq, 0:Sk], scalar1=tcur[0:nq, 0:1], scalar2=None, op0=ALU.is_ge), r=[scores, tcur], w=[maskb, maskb_hi])

        def kblocks_of(ti):
            Sk = tiles[ti]["nkeys"]
            return [(k0, min(128, Sk - k0)) for k0 in range(0, Sk, 128)]

        def MT(ti):
            t = tiles[ti]
            nq = t["nq"]
            kblocks = kblocks_of(ti)
            nkb = len(kblocks)
            j = 0
            while j < nkb:
                nj = min(8, nkb - j)
                bk = bank("a")
                for jj in range(nj):
                    k0, ks = kblocks[j + jj]
                    S.op("pe", lambda e: e.transpose(out=bf(bk)[0:ks, jj * 128:jj * 128 + nq], in_=maskb[0:nq, k0:k0 + ks], identity=ident[0:nq, 0:nq]),
                         r=[maskb, ident], w=[bk], signal=(jj == nj - 1))
                ksz = kblocks[j + nj - 1][1]
                nfull = nj if ksz == 128 else nj - 1
                if nfull > 0:
                    S.op("act", lambda e: e.activation(out=maskT[:, j * 128:(j + nfull) * 128].rearrange("p (b q) -> p b q", b=nfull)[:, :, 0:nq],
                                                       in_=bf(bk)[:, 0:nfull * 128].rearrange("p (b q) -> p b q", b=nfull)[:, :, 0:nq], func=AF.Copy), r=[bk], w=[maskT])
                if ksz != 128:
                    S.op("act", lambda e: e.activation(out=maskT[0:ksz, (j + nj - 1) * 128:(j + nj - 1) * 128 + nq], in_=bf(bk)[0:ksz, (nj - 1) * 128:(nj - 1) * 128 + nq], func=AF.Copy), r=[bk], w=[maskT])
                j += nj

        def AT(ti, per):
            t = tiles[ti]
            nq = t["nq"]
            kT_, V_, _ = t["cache"][l]
            kblocks = kblocks_of(ti)
            nkb = len(kblocks)
            for n in range(2):
                S.op("dve", lambda e: e.memset(Ob[n][:, :], 0.0), w=[Ob[n]])
            its = [(kb_i, k0, ks, n) for kb_i, (k0, ks) in enumerate(kblocks) for n in range(2)]

            def emit_lg(kb_i, k0, ks, n):
                lg = bank("b")
                S.op("pe", lambda e: e.matmul(lg[0:ks, 0:4 * nq], lhsT=kT_[64 * n:64 * n + 64, k0:k0 + ks], rhs=QT[64 * n:64 * n + 64, ti, 0:4 * nq], start=True, stop=True),
                     r=[kT_, QT], w=[lg])
                return lg

            def emit_E(x, lg):
                kb_i, k0, ks, n = x
                pt = ring(PTr, ptc)
                S.op("act", lambda e: e.activation(out=pt[0:ks, 0:4 * nq], in_=lg[0:ks, 0:4 * nq], func=AF.Exp), r=[lg], w=[pt])
                return pt

            def emit_M(x, pt):
                kb_i, k0, ks, n = x
                S.op(MULT_ENG, lambda e: e.tensor_tensor(out=pt[0:ks, 0:4 * nq].rearrange("p (g q) -> p g q", g=4), in0=pt[0:ks, 0:4 * nq].rearrange("p (g q) -> p g q", g=4),
                                                         in1=maskT[0:ks, kb_i * 128:kb_i * 128 + nq].unsqueeze(1).broadcast_to([ks, 4, nq]), op=ALU.mult), r=[pt, maskT], w=[pt])
                for g in range(4):
                    S.op("pe", lambda e, g=g: e.matmul(Ob[n][0:nq, g * 65:(g + 1) * 65], lhsT=pt[0:ks, g * nq:(g + 1) * nq], rhs=V_[0:ks, kb_i, n, :], start=False, stop=(kb_i == nkb - 1), skip_group_check=True),
                         r=[pt, V_], w=[Ob[n]], signal=(g == 3))

            chunks = [its[i:i + per] for i in range(0, len(its), per)]
            look = max(1, 6 // per - 1)
            lgq = {}
            for i in range(min(look, len(chunks))):
                lgq[i] = [emit_lg(*x) for x in chunks[i]]
            for ci, ch in enumerate(chunks):
                if ci + look < len(chunks):
                    lgq[ci + look] = [emit_lg(*x) for x in chunks[ci + look]]
                pts = [emit_E(x, lg) for x, lg in zip(ch, lgq.pop(ci))]
                yield "E"
                for x, pt in zip(ch, pts):
                    emit_M(x, pt)
                yield "M"

        def FIN(ti):
            t = tiles[ti]
            nq, col = t["nq"], t["col"]
            rd = ring(smr, smc)
            on = ring(b16r, b16c)
            for n in range(2):
                ov = Ob[n][0:nq, 0:260].rearrange("p (g c) -> p g c", g=4)
                S.op("dve", lambda e: e.reciprocal(out=rd[0:nq, 4 * n:4 * n + 4], in_=ov[:, :, 64]), r=[Ob[n]], w=[rd])
                S.op("dve", lambda e: e.tensor_tensor(out=on[0:nq, 256 * n:256 * n + 256].rearrange("p (g d) -> p g d", g=4), in0=ov[:, :, 0:64],
                                                      in1=rd[0:nq, 4 * n:4 * n + 4].unsqueeze(2).broadcast_to([nq, 4, 64]), op=ALU.mult), r=[Ob[n], rd], w=[on])
            transposes(on, lambda j: on[0:nq, j * 128:(j + 1) * 128], 4, nq,
                       lambda e, bv, j, nj: e.tensor_copy(out=OaT[:, 0:4, col:col + nq], in_=bv[:, 0:512].rearrange("p (j q) -> p j q", j=4)[:, :, 0:nq]), [OaT])

        def per_of(ti):
            n_at = 2 * len(kblocks_of(ti))
            return max(1, min(3, -(-n_at // (NIT + 1))))

        prev = None
        for ti in range(NT):
            SC(ti)
            at = AT(prev, per_of(prev)) if prev is not None else None
            if at is not None:
                next(at, None)
            for _tag in BI(ti):
                if at is not None:
                    next(at, None)
            if at is not None:
                for _ in at:
                    pass
                FIN(prev)
            MT(ti)
            prev = ti
        for _ in AT(prev, 2):
            pass
        FIN(prev)

        stage(4)
        wpa = [wchunk("w_pa", l, 0, cb * 512, 512) for cb in range(2)]
        wpb = [wchunk("w_pb", l, 0, cb * 512, 512) for cb in range(2)]
        for ti, t in enumerate(tiles):
            nq, col = t["nq"], t["col"]
            mg = ring(b16r, b16c)
            for cb in range(2):
                pa = bank("all")
                for kc in range(4):
                    S.op("pe", lambda e, pa=pa, kc=kc, cb=cb: e.matmul(pa[0:nq, :], lhsT=OaT[:, kc, col:col + nq], rhs=wpa[cb][:, kc, :], start=(kc == 0), stop=(kc == 3)), r=[OaT, wpa[cb]], w=[pa], signal=(kc == 3))
                pb = bank("all")
                for kc in range(4):
                    S.op("pe", lambda e, pb=pb, kc=kc, cb=cb: e.matmul(pb[0:nq, :], lhsT=ObT[:, kc, col:col + nq], rhs=wpb[cb][:, kc, :], start=(kc == 0), stop=(kc == 3)), r=[ObT, wpb[cb]], w=[pb], signal=(kc == 3))
                m1 = ring(f32r, f32c)
                m2 = ring(f32r, f32c)
                S.op("dve", lambda e, pa=pa, m1=m1, cb=cb: e.tensor_tensor(out=m1[0:nq, :], in0=pa[0:nq, :], in1=gates[0:nq, ti, cb * 512:(cb + 1) * 512], op=ALU.mult), r=[pa, gates], w=[m1])
                S.op("dve", lambda e, pb=pb, m2=m2, cb=cb: e.tensor_tensor(out=m2[0:nq, :], in0=pb[0:nq, :], in1=gates[0:nq, ti, 1024 + cb * 512:1024 + (cb + 1) * 512], op=ALU.mult), r=[pb, gates], w=[m2])
                S.op("pool", lambda e, m1=m1, m2=m2, mg=mg, cb=cb: e.tensor_tensor(out=mg[0:nq, cb * 512:(cb + 1) * 512], in0=m1[0:nq, :], in1=m2[0:nq, :], op=ALU.add), r=[m1, m2], w=[mg])
            transposes(mg, lambda j, mg=mg: mg[0:nq, j * 128:(j + 1) * 128], 8, nq,
                       lambda e, bv, j, nj, col=col, nq=nq: e.tensor_copy(out=actT[:, j:j + nj, col:col + nq], in_=bv[:, 0:nj * 128].rearrange("p (j q) -> p j q", j=nj)[:, :, 0:nq]), [actT])
        for cb in range(2):
            wo = [wchunk("w_out", l, r, cb * 512, 512) for r in range(2)]
            for ti, t in enumerate(tiles):
                nq, col = t["nq"], t["col"]
                acc = bank("all")
                for kc in range(8):
                    S.op("pe", lambda e, acc=acc, kc=kc: e.matmul(acc[0:nq, :], lhsT=actT[:, kc, col:col + nq], rhs=wo[kc // 4][:, kc % 4, :], start=(kc == 0), stop=(kc == 7)), r=[actT, wo[kc // 4]], w=[acc], signal=(kc == 7 or kc == 3))
                S.op("dve", lambda e, acc=acc, ti=ti, cb=cb, nq=nq: e.scalar_tensor_tensor(out=xres[0:nq, ti, cb * 512:(cb + 1) * 512], in0=xres[0:nq, ti, cb * 512:(cb + 1) * 512], scalar=ALPHA, in1=acc[0:nq, :], op0=ALU.mult, op1=ALU.add), r=[acc, xrt[ti]], w=[xrt[ti]])
        rr([ln_xT_g(ti, t, True) for ti, t in enumerate(tiles)])
        S.dma("sp", p_ln[:, 0, :], ln2g[l:l + 1, :].partition_broadcast(128), lsem["ln"], w=[p_ln])
        S.dma("sp", p_ln[:, 1, :], ln2b[l:l + 1, :].partition_broadcast(128), lsem["ln"], w=[p_ln])
        for fb in range(8):
            w1 = [wchunk("w_ff1", l, r, fb * 512, 512) for r in range(2)]
            for fc in range(4):
                acc = bank("all")
                for kc in range(8):
                    S.op("pe", lambda e, acc=acc, kc=kc, fc=fc: e.matmul(acc[:, 0:TT], lhsT=w1[kc // 4][:, kc % 4, fc * 128:(fc + 1) * 128], rhs=actT[:, kc, 0:TT], start=(kc == 0), stop=(kc == 7)), r=[actT, w1[kc // 4]], w=[acc], signal=(kc == 7 or kc == 3))
                rl = ring(f32r, f32c)
                S.op("act", lambda e, acc=acc, rl=rl: e.activation(out=rl[:, 0:TT], in_=acc[:, 0:TT], func=AF.Relu), r=[acc], w=[rl])
                S.op("dve", lambda e, rl=rl, fb=fb, fc=fc: e.tensor_tensor(out=HT[:, (fb * 4 + fc) * 512:(fb * 4 + fc) * 512 + TT], in0=rl[:, 0:TT], in1=rl[:, 0:TT], op=ALU.mult), r=[rl], w=[HT])
        for cb in range(2):
            accs = [banks[4 + ti] for ti in range(NT)]
            for r in range(8):
                w2 = wchunk("w_ff2", l, r, cb * 512, 512)
                for ti, t in enumerate(tiles):
                    nq, col = t["nq"], t["col"]
                    for k4 in range(4):
                        S.op("pe", lambda e, ti=ti, k4=k4, r=r, nq=nq, col=col, w2=w2: e.matmul(accs[ti][0:nq, :], lhsT=HT[:, (4 * r + k4) * 512 + col:(4 * r + k4) * 512 + col + nq], rhs=w2[:, k4, :], start=(r == 0 and k4 == 0), stop=(r == 7 and k4 == 3)),
                             r=[HT, w2], w=[accs[ti]], signal=(k4 == 3))
            for ti, t in enumerate(tiles):
                nq = t["nq"]
                S.op("dve", lambda e, ti=ti, cb=cb, nq=nq: e.scalar_tensor_tensor(out=xres[0:nq, ti, cb * 512:(cb + 1) * 512], in0=xres[0:nq, ti, cb * 512:(cb + 1) * 512], scalar=ALPHA, in1=accs[ti][0:nq, :], op0=ALU.mult, op1=ALU.add), r=[accs[ti], xrt[ti]], w=[xrt[ti]])
        rr([ln_xT_g(ti, t, not last) for ti, t in enumerate(tiles)])
        if last:
            for ti, t in enumerate(tiles):
                S.dma("sp", t["outs"]["y"], xres[0:t["nq"], ti, :], osem["y"], r=[xrt[ti]])
        return not last

    pcache = {l: (KT[l], Vc[l], KiT[l]) for l in range(NL)}
    try:
      stage(1)
      for j in range(NST):
          tok0 = j * 512
          S.dma("sp", xres[:, :, :], xp[tok0:tok0 + 512, :].rearrange("(t p) d -> p t d", p=128), xsem, w=[xres])
          tiles = []
          for t in range(4):
              g = j * 4 + t
              p0 = g * 128
              tiles.append(dict(topk=min(TOPK, SEQ // 4), nq=128, col=t * 128, tix=g, cache=pcache, kpos=p0, nkeys=p0 + 128, causal=True,
                                outs=dict(k=[pk[l, p0:p0 + 128, :] for l in range(NL)], v=[pv[l, p0:p0 + 128, :] for l in range(NL)],
                                          ki=[pik[l, p0:p0 + 128, :] for l in range(NL)], y=yp[p0:p0 + 128, :])))
          ready = False
          for l in range(NL):
              ready = layer_pass(tiles, l, l == NL - 1, ready)

      S.dma("sp", xres[0:32, 0:2, :], xs.rearrange("(t p) d -> p t d", p=32), xsem, w=[xres])
      ready_s = False
      for l in range(NL):
          scache = {}
          for s in range(2):
              kT_, V_, KiT_ = KT[s], Vc[s], KiT[s]
              for (src, width, kind) in ((ck, 128, "k"), (cv, 128, "v"), (cik, 64, "ki")):
                  S.dma("sp", scores[:, 0:PBLK * width].rearrange("p (b c) -> p b c", b=PBLK), src[l, s].rearrange("(b p) c -> p b c", p=128), stsem, w=[scores])
                  if kind == "v":
                      S.op("dve", lambda e, V_=V_: e.tensor_copy(out=V_[:, 0:PBLK, :, 0:64], in_=scores[:, 0:PBLK * 128].rearrange("p (b n d) -> p b n d", b=PBLK, n=2)), r=[scores], w=[V_])
                      continue
                  for b0 in range(0, PBLK, 8):
                      nb = min(8, PBLK - b0)
                      cb16 = ring(b16r, b16c)
                      if kind == "k":
                          S.op("dve", lambda e, cb16=cb16, b0=b0, nb=nb: e.tensor_copy(out=cb16[:, 0:nb * 128], in_=scores[:, b0 * 128:(b0 + nb) * 128]), r=[scores], w=[cb16])
                          dstT = kT_
                      else:
                          S.op("dve", lambda e, cb16=cb16, b0=b0, nb=nb: e.tensor_copy(out=cb16[:, 0:nb * 128].rearrange("p (b r d) -> p b r d", b=nb, r=2),
                                                                               in_=scores[:, b0 * 64:(b0 + nb) * 64].rearrange("p (b d) -> p b d", b=nb).unsqueeze(2).broadcast_to([128, nb, 2, 64])), r=[scores], w=[cb16])
                          dstT = KiT_
                      transposes(cb16, lambda j, cb16=cb16: cb16[:, j * 128:(j + 1) * 128], nb, 128,
                                 lambda e, bv, j, nj, dstT=dstT, b0=b0: e.tensor_copy(out=dstT[:, (b0 + j) * 128:(b0 + j + nj) * 128], in_=bv[:, 0:nj * 128]), [dstT])
              scache[s] = {l: (kT_, V_, KiT_)}
          tiles = []
          for s in range(2):
              tiles.append(dict(topk=min(TOPK, (PAST + 32) // 4), nq=32, col=s * 32, tix=NTT, cache=scache[s], kpos=PAST, nkeys=PAST + 32, causal=False,
                                outs=dict(k=[sk[ll, 32 * s:32 * s + 32, :] for ll in range(NL)], v=[sv[ll, 32 * s:32 * s + 32, :] for ll in range(NL)],
                                          ki=[sik[ll, 32 * s:32 * s + 32, :] for ll in range(NL)], vn=[ssv[ll, 32 * s:32 * s + 32, :] for ll in range(NL)],
                                          y=ys[32 * s:32 * s + 32, :])))
          ready_s = layer_pass(tiles, l, l == NL - 1, ready_s)
    except _Stop:
        pass

    S.final_waits("sp")
    print("instructions per engine:", S.ninst, "signals:", {k: v.n for k, v in S.E.items()}, flush=True)
    es.close()
    return nc


def _consts(SEQ, PAST):
    NTT = SEQ // 128
    half = 8
    freqs = (500000.0 ** (-np.arange(half, dtype=np.float32) * 2.0 / 16)).astype(np.float32)
    rope = np.zeros((128, NTT + 1, 32), np.float32)
    for ti in range(NTT + 1):
        pos = (np.arange(128) + (ti * 128 if ti < NTT else PAST)).astype(np.float32)
        ang = (pos[:, None] * freqs[None, :]).astype(np.float32)
        c, s = np.cos(ang).astype(np.float32), np.sin(ang).astype(np.float32)
        rope[:, ti, 0:8] = c
        rope[:, ti, 8:16] = c
        rope[:, ti, 16:24] = -s
        rope[:, ti, 24:32] = s
    steps = np.zeros((128, NIT + 2), np.float32)
    for i in range(NIT + 1):
        steps[:, i] = 2.0 * 1.5 * 2.0 ** (-i)
    steps[:, NIT + 1] = 1.5 * 2.0 ** (-NIT)
    ident = np.eye(128, dtype=np.float32).astype(ml_dtypes.bfloat16)
    cm = (np.arange(128)[None, :] >= np.arange(128)[:, None]).astype(np.float32).astype(ml_dtypes.bfloat16)
    return dict(c_ident=ident, c_cmask=cm, c_rope=rope, c_steps=steps)


_NC_CACHE = {}


def run(inputs, SEQ, PAST, n_cores=8, stop=0):
    key = (SEQ, PAST, stop)
    if key not in _NC_CACHE:
        _NC_CACHE[key] = build(SEQ, PAST, stop)
    nc = _NC_CACHE[key]
    cs = _consts(SEQ, PAST)
    f = lambda a: np.ascontiguousarray(np.asarray(a, dtype=np.float32))
    wnames = ["w_in", "w_pa", "w_pb", "w_out", "w_ff1", "w_ff2", "idx_k_g", "idx_k_b", "sgu_ln_g", "sgu_ln_b", "w_s", "b_s",
              "ln1_g", "ln1_b", "ln2_g", "ln2_b"]
    shared = {k: f(inputs[k]) for k in wnames}
    xp = f(inputs["x_prompt"]); xs = f(inputs["x_sample"])
    ck = f(inputs["cache_k"]); cv = f(inputs["cache_v"]); cik = f(inputs["cache_idx_k"])
    in_maps = []
    for c in range(n_cores):
        m = dict(shared)
        m.update(cs)
        m["xp"] = xp[c]
        m["xs"] = xs[2 * c:2 * c + 2].reshape(64, D)
        m["ck"] = np.ascontiguousarray(ck[:, 2 * c:2 * c + 2].reshape(NL, 2, PAST, 128))
        m["cv"] = np.ascontiguousarray(cv[:, 2 * c:2 * c + 2].reshape(NL, 2, PAST, 128))
        m["cik"] = np.ascontiguousarray(cik[:, 2 * c:2 * c + 2])
        in_maps.append(m)
    res = run_bass_kernel_spmd(nc, in_maps, core_ids=list(range(n_cores)))
    R = res.results
    B = n_cores
    y_p = np.stack([R[c]["yp"] for c in range(B)], 0)
    y_s = np.concatenate([R[c]["ys"].reshape(2, 32, D) for c in range(B)], 0)
    p_k = np.stack([R[c]["pk"] for c in range(B)], 1).reshape(NL, B, SEQ, 2, 64)
    p_v = np.stack([R[c]["pv"] for c in range(B)], 1).reshape(NL, B, SEQ, 2, 64)
    p_ik = np.stack([R[c]["pik"] for c in range(B)], 1)
    s_k = np.concatenate([R[c]["sk"].reshape(NL, 2, 32, 2, 64) for c in range(B)], 1)
    s_v = np.concatenate([R[c]["sv"].reshape(NL, 2, 32, 2, 64) for c in range(B)], 1)
    s_ik = np.concatenate([R[c]["sik"].reshape(NL, 2, 32, 64) for c in range(B)], 1)
    s_sv = np.concatenate([R[c]["ssv"].reshape(NL, 2, 32, 512) for c in range(B)], 1)
    return tuple(np.asarray(a, dtype=np.float32) for a in (y_p, y_s, p_k, p_v, p_ik, s_k, s_v, s_ik, s_sv))


def kernel(**inputs):
    SEQ = int(np.asarray(inputs["x_prompt"]).shape[1])
    PAST = int(np.asarray(inputs["cache_k"]).shape[2])
    return run(inputs, SEQ, PAST, 8)
```

```python
from contextlib import ExitStack
import numpy as np
import ml_dtypes
import concourse.bass as bass
import concourse.mybir as mybir
from concourse.bass_utils import run_bass_kernel_spmd

F32 = mybir.dt.float32
BF16 = mybir.dt.bfloat16
ALU = mybir.AluOpType
AF = mybir.ActivationFunctionType
AX = mybir.AxisListType

D = 1024
NL = 2
DFF = 4096
IN_DIM = 4164
TOPK = 256
NIT = 18
ALPHA = (2 * NL) ** 0.25
EPS = 1e-5
NEG = -1.0e30
EP = 30000
MULT_ENG = "dve"
SPLIT = 0.45


class Trk:
    __slots__ = ("w", "r")

    def __init__(self):
        self.w = None
        self.r = []


class Tile:
    def __init__(self, t, trks=None, excl=False):
        self.t = t
        self.trks = trks if trks is not None else [Trk()]
        self.excl = excl

    def __getitem__(self, k):
        return self.t[k]


class DmaSem:
    def __init__(self, h, idx):
        self.h = h
        self.idx = idx
        self.val = 0


class Eng:
    def __init__(self, name, sems, selfdep=True):
        self.name = name
        self.sems = sems
        self.n = 0
        self.waited = {}
        self.items = []
        self.selfdep = selfdep
        self.pending = False


class Sched:
    def __init__(self, nc, es):
        self.nc = nc
        self.es = es
        self.nsem = 0

        def mk(n):
            out = []
            for _ in range(n):
                out.append(es.enter_context(nc.semaphore(f"s{self.nsem}")))
                self.nsem += 1
            return out

        self.E = {
            "pe": Eng("pe", mk(3), selfdep=False),
            "act": Eng("act", mk(3)),
            "dve": Eng("dve", mk(3)),
            "pool": Eng("pool", mk(2)),
            "sp": Eng("sp", []),
        }
        self.dsems = []
        self.hw = {"pe": nc.tensor, "act": nc.scalar, "dve": nc.vector, "pool": nc.gpsimd, "sp": nc.sync}
        self.ninst = {}
        self.single = {}

    def dma_sem(self):
        h = self.es.enter_context(self.nc.semaphore(f"d{self.nsem}"))
        self.nsem += 1
        s = DmaSem(h, len(self.dsems))
        self.dsems.append(s)
        return s

    def _waits(self, eng, r, w):
        toks = []
        for tl in r:
            for k in tl.trks:
                if k.w is not None:
                    toks.append(k.w)
                if tl.excl:
                    toks.extend(tk for tk in k.r if not (tk[0] == "e" and tk[1] == eng))
        for tl in w:
            for k in tl.trks:
                if k.w is not None:
                    toks.append(k.w)
                toks.extend(k.r)
        e = self.E[eng]
        best = {}
        for tok in toks:
            if tok[0] == "e":
                _, en, n = tok
                if en == eng and not e.selfdep:
                    continue
                if n > self.E[en].n:
                    raise RuntimeError(f"token of {en} not signalled yet ({n}>{self.E[en].n})")
                key = en
                val = n
            else:
                _, ds, val = tok
                key = ("d", ds.idx)
            if val > best.get(key, 0):
                best[key] = val
        waits = []
        for key, val in best.items():
            if e.waited.get(key, 0) >= val:
                continue
            e.waited[key] = val
            if isinstance(key, str):
                waits.append((self.E[key].sems[(val - 1) // EP], (val - 1) % EP + 1))
            else:
                waits.append((self.dsems[key[1]].h, val))
        return waits

    def op(self, eng, fn, r=(), w=(), signal=True):
        e = self.E[eng]
        waits = self._waits(eng, r, w)
        if signal:
            e.n += 1
            n = e.n
            inc = (e.sems[(n - 1) // EP], 1)
            tok = ("e", eng, n)
            e.pending = False
        else:
            inc = None
            tok = ("e", eng, e.n + 1)
            e.pending = True
        self._emit(eng, waits, fn, inc)
        for tl in r:
            for k in tl.trks:
                k.r.append(tok)
        for tl in w:
            for k in tl.trks:
                k.w = tok
                k.r = []
        return tok

    def _emit(self, eng, waits, fn, inc):
        en = self.hw[eng]
        if fn is None:
            for (h, v) in waits:
                en.wait_ge(h, v)
            return
        key = (eng, fn.__code__.co_filename, fn.__code__.co_firstlineno) if eng in ("act", "dve", "pool") else None
        fuse = bool(waits) and key is not None and self.single.get(key, False)
        sep = waits[:-1] if fuse else waits
        for (h, v) in sep:
            en.wait_ge(h, v)
        n0 = self.nc.n_instructions()
        ins = fn(en)
        n1 = self.nc.n_instructions()
        if key is not None and key not in self.single:
            self.single[key] = (n1 - n0 == 1)
        if fuse:
            assert n1 - n0 == 1, key
            ins._wait_ge(waits[-1][0], waits[-1][1])
        if inc is not None:
            ins.then_inc(inc[0], inc[1])
        self.ninst[eng] = self.ninst.get(eng, 0) + 1 + len(sep)

    def dma(self, q, out, in_, sem, r=(), w=(), **kw):
        e = self.E[q]
        waits = self._waits(q, r, w)
        sem.val += 16
        tok = ("d", sem, sem.val)
        self._emit(q, waits, lambda en: en.dma_start(out=out, in_=in_, **kw), (sem.h, 16))
        for tl in r:
            for k in tl.trks:
                k.r.append(tok)
        for tl in w:
            for k in tl.trks:
                k.w = tok
                k.r = []
        return tok

    def final_waits(self, q):
        e = self.E[q]
        waits = []
        for ds in self.dsems:
            if ds.val > 0 and e.waited.get(("d", ds.idx), 0) < ds.val:
                waits.append((ds.h, ds.val))
        for en, src in self.E.items():
            if en == q or not src.sems or src.n == 0:
                continue
            n = src.n
            waits.append((src.sems[(n - 1) // EP], (n - 1) % EP + 1))
        self._emit(q, waits, None, None)

    def replay(self):
        nc = self.nc
        engmap = {"pe": "tensor", "act": "scalar", "dve": "vector", "pool": "gpsimd", "sp": "sync"}
        with nc.Block() as block:
            for name, e in self.E.items():
                items = e.items

                def body(en, items=items):
                    for waits, fn, inc in items:
                        for (h, v) in waits:
                            en.wait_ge(h, v)
                        if fn is None:
                            continue
                        ins = fn(en)
                        if inc is not None:
                            ins.then_inc(inc[0], inc[1])

                getattr(block, engmap[name])(body)


class _Stop(Exception):
    pass


def build(SEQ, PAST, stop=0):
    nc = bass.Bass("TRN2", target_bir_lowering=False)
    es = ExitStack()
    S = Sched(nc, es)
    NTT = SEQ // 128
    NST = SEQ // 512
    SMAX = max(SEQ, ((PAST + 32 + 127) // 128) * 128)
    NBLK = SMAX // 128
    PBLK = PAST // 128

    def din(name, shape, dt=F32):
        return nc.dram_tensor(name, list(shape), dt, kind="ExternalInput").ap()

    def dout(name, shape, dt=F32):
        return nc.dram_tensor(name, list(shape), dt, kind="ExternalOutput").ap()

    xp = din("xp", [SEQ, D])
    xs = din("xs", [64, D])
    ck = din("ck", [NL, 2, PAST, 128])
    cv = din("cv", [NL, 2, PAST, 128])
    cik = din("cik", [NL, 2, PAST, 64])
    WSH = {"w_in": [NL, D, IN_DIM], "w_pa": [NL, 512, D], "w_pb": [NL, 512, D], "w_out": [NL, D, D],
           "w_ff1": [NL, D, DFF], "w_ff2": [NL, DFF, D]}
    wf = {k: din(k, v) for k, v in WSH.items()}
    wb = {k: nc.dram_tensor(k + "_b", list(v), BF16, kind="Internal").ap() for k, v in WSH.items()}
    idx_g = din("idx_k_g", [NL, 64]); idx_b = din("idx_k_b", [NL, 64])
    sgu_g = din("sgu_ln_g", [NL, 512]); sgu_b = din("sgu_ln_b", [NL, 512])
    w_s = din("w_s", [NL, 4, 128, 128]); b_s = din("b_s", [NL, 4, 128])
    ln1g = din("ln1_g", [NL, D]); ln1b = din("ln1_b", [NL, D])
    ln2g = din("ln2_g", [NL, D]); ln2b = din("ln2_b", [NL, D])
    c_ident = din("c_ident", [128, 128], BF16)
    c_cmask = din("c_cmask", [128, 128], BF16)
    c_rope = din("c_rope", [128, NTT + 1, 32])
    c_steps = din("c_steps", [128, NIT + 2])

    yp = dout("yp", [SEQ, D]); ys = dout("ys", [64, D])
    pk = dout("pk", [NL, SEQ, 128]); pv = dout("pv", [NL, SEQ, 128]); pik = dout("pik", [NL, SEQ, 64])
    sk = dout("sk", [NL, 64, 128]); sv = dout("sv", [NL, 64, 128]); sik = dout("sik", [NL, 64, 64])
    ssv = dout("ssv", [NL, 64, 512])

    def sb(name, shape, dt, trks=None):
        return Tile(es.enter_context(nc.sbuf_tensor(name, list(shape), dt)), trks)

    ident = sb("ident", [128, 128], BF16)
    cmask = sb("cmask", [128, 128], BF16)
    rope = sb("rope", [128, NTT + 1, 32], F32)
    steps = sb("steps", [128, NIT + 2], F32)
    neghalf = sb("neghalf", [128, 1], F32)
    WsT = [sb(f"WsT{l}", [128, 4, 128], BF16) for l in range(NL)]
    bsT = [sb(f"bsT{l}", [128, 4], F32) for l in range(NL)]
    KT = [sb(f"KT{l}", [128, SMAX], BF16) for l in range(NL)]
    Vc = [sb(f"V{l}", [128, NBLK, 2, 65], BF16) for l in range(NL)]
    KiT = [sb(f"KiT{l}", [128, SMAX], BF16) for l in range(NL)]
    p_ln = sb("p_ln", [128, 2, D], F32)
    p_sgu = sb("p_sgu", [128, 2, 512], F32)
    p_idx = sb("p_idx", [128, 2, 64], F32)
    xres = sb("xres", [128, 4, D], F32, trks=[Trk() for _ in range(4)])
    xrt = [Tile(xres.t, [xres.trks[i]]) for i in range(4)]
    actT = sb("actT", [128, 8, 512], BF16)
    QT = sb("QT", [128, 4, 512], BF16)
    QiT = sb("QiT", [128, 4, 2, 128], BF16)
    absw = sb("absw", [128, 4, 4], F32)
    sgnw = sb("sgnw", [128, 4, 4], F32)
    ub = sb("ub", [128, 4, 512], BF16)
    ObT = sb("ObT", [128, 4, 512], BF16)
    OaT = sb("OaT", [128, 4, 512], BF16)
    gates = sb("gates", [128, 4, 2048], BF16)
    big = es.enter_context(nc.sbuf_tensor("big", [128, 8192], F32))
    t_sc, t_mk, t_mt = Trk(), Trk(), Trk()
    scores = Tile(big[:, 0:4096], [t_sc])
    bigb = big[:].bitcast(BF16)
    maskb = Tile(bigb[:, 8192:12288], [t_mk])
    maskT = Tile(bigb[:, 12288:16384], [t_mt])
    maskb_hi = Tile(bigb[:, 8192:12288], [Trk()])
    HT = Tile(bigb, [t_sc, t_mk, t_mt])
    NW = 5
    wring = [sb(f"wr{i}", [128, 4, 512], BF16) for i in range(NW)]
    wsem = [S.dma_sem() for _ in range(NW)]
    wcnt = [0]
    PTr = [sb(f"PT{i}", [128, 512], BF16) for i in range(4)]
    ptc = [0]
    f32r = [sb(f"f32r{i}", [128, 512], F32) for i in range(4)]
    f32c = [0]
    b16r = [sb(f"b16r{i}", [128, 1024], BF16) for i in range(4)]
    b16c = [0]
    smr = [sb(f"smr{i}", [128, 128], F32) for i in range(12)]
    smc = [0]
    kout = [sb(f"kout{i}", [128, 128], F32) for i in range(2)]
    vout = [sb(f"vout{i}", [128, 128], F32) for i in range(2)]
    kiout = [sb(f"kiout{i}", [128, 64], F32) for i in range(2)]
    osem = {k: S.dma_sem() for k in ["k0", "k1", "v0", "v1", "ki0", "ki1", "y"]}
    vnsem = [S.dma_sem() for _ in range(4)]
    oc = [0]
    oc2 = [0]
    tcur = sb("tcur", [128, 2], F32)
    stp = sb("stp", [128, NIT + 2], F32)
    cntt = sb("cntt", [128, 2], F32)
    cnta = sb("cnta", [128, 2], F32)
    bmax = sb("bmax", [128, 8], F32)
    dcur = sb("dcur", [128, 2], F32)
    ntc = sb("ntc", [128, 2], F32)
    nstp = sb("nstp", [128, NIT + 2], F32)

    def ring(lst, c):
        t = lst[c[0] % len(lst)]
        c[0] += 1
        return t

    banks = [Tile(es.enter_context(nc.psum_tensor(f"ps{i}", [128, 512], F32)), excl=True) for i in range(8)]
    bcnt = {k: [0] for k in ["a", "b", "all", "ip", "s"]}
    bpool = {"a": [0, 1, 2], "b": [0, 1, 2, 3, 4, 5], "all": [0, 1, 2, 3, 4, 5, 6, 7], "ip": [4, 5, 6, 7], "s": [2, 3]}

    def bank(pool="a"):
        lst = bpool[pool]
        i = lst[bcnt[pool][0] % len(lst)]
        bcnt[pool][0] += 1
        return banks[i]

    def bf(bk):
        return bk.t[:].bitcast(BF16)

    psem = {(k, l): S.dma_sem() for k in WSH for l in range(NL)}
    xsem = S.dma_sem()
    stsem = S.dma_sem()
    lsem = {k: S.dma_sem() for k in ["ln", "sgu", "idx"]}

    S.dma("sp", ident[:], c_ident[:, :], S.dma_sem(), w=[ident])
    S.dma("sp", cmask[:], c_cmask[:, :], S.dma_sem(), w=[cmask])
    S.dma("sp", rope[:], c_rope[:, :, :], S.dma_sem(), w=[rope])
    S.dma("sp", steps[:], c_steps[:, :], S.dma_sem(), w=[steps])
    S.op("dve", lambda e: e.memset(neghalf[:], -0.5), w=[neghalf])
    wtile = {(k, l): Tile(None) for k in WSH for l in range(NL)}
    for l in range(NL):
        for k, shp in WSH.items():
            rows = shp[1]
            nsplit = max(1, rows // 256)
            for i in range(nsplit):
                r0 = i * rows // nsplit
                r1 = (i + 1) * rows // nsplit
                S.dma("pool", wb[k][l, r0:r1, :], wf[k][l, r0:r1, :], psem[(k, l)], w=[wtile[(k, l)]])
    for l in range(NL):
        S.op("dve", lambda e, l=l: e.memset(Vc[l][:, :, :, 64:65], 1.0), w=[Vc[l]])
        st = ring(f32r, f32c)
        S.dma("sp", st[:, 0:512].rearrange("p (g s) -> p g s", g=4), w_s[l].rearrange("g t s -> t g s"), S.dma_sem(), w=[st])
        sbt = ring(b16r, b16c)
        S.op("dve", lambda e, st=st, sbt=sbt: e.tensor_copy(out=sbt[:, 0:512], in_=st[:, 0:512]), r=[st], w=[sbt])
        bk = bank("a")
        for g in range(4):
            S.op("pe", lambda e, g=g, bk=bk, sbt=sbt: e.transpose(out=bf(bk)[:, g * 128:(g + 1) * 128], in_=sbt[:, g * 128:(g + 1) * 128], identity=ident[:]),
                 r=[sbt, ident], w=[bk], signal=(g == 3))
        S.op("dve", lambda e, l=l, bk=bk: e.tensor_tensor(out=WsT[l][:], in0=bf(bk)[:, 0:512].rearrange("p (g t) -> p g t", g=4),
                                                      in1=cmask[:].unsqueeze(1).broadcast_to([128, 4, 128]), op=ALU.mult),
             r=[bk, cmask], w=[WsT[l]])
        S.dma("sp", bsT[l][:], b_s[l].rearrange("g t -> t g"), S.dma_sem(), w=[bsT[l]], allow_slow_non_contiguous=True)

    def stage(k):
        if stop and k >= stop:
            raise _Stop()

    def wchunk(name, l, r, c0, ncols):
        i = wcnt[0] % NW
        wcnt[0] += 1
        slot = wring[i]
        src = wb[name][l, 512 * r:512 * r + 512, c0:c0 + ncols].rearrange("(kc p) n -> p kc n", p=128)
        S.dma("sp", slot[:, :, 0:ncols], src, wsem[i], r=[wtile[(name, l)]], w=[slot])
        return slot

    def run(g):
        for _ in g:
            pass

    def rr(gens):
        gens = list(gens)
        while gens:
            for g in list(gens):
                try:
                    next(g)
                except StopIteration:
                    gens.remove(g)

    def transposes_g(src_tile, src_ap_fn, nblk, nq, dst_fn, dst_tiles, ncols=128, evac="dve", pool="a"):
        j = 0
        while j < nblk:
            nj = min(8, nblk - j)
            bk = bank(pool)
            for jj in range(nj):
                S.op("pe", lambda e, bk=bk, jj=jj, j=j: e.transpose(out=bf(bk)[0:ncols, jj * 128:jj * 128 + nq], in_=src_ap_fn(j + jj), identity=ident[0:nq, 0:nq]),
                     r=[src_tile, ident], w=[bk], signal=(jj == nj - 1))
            yield
            S.op(evac, lambda e, bk=bk, j=j, nj=nj: dst_fn(e, bf(bk), j, nj), r=[bk], w=dst_tiles)
            yield
            j += nj

    def transposes(*a, **k):
        run(transposes_g(*a, **k))

    def rope_g(buf, c0, H, nq, tix):
        x = buf[0:nq, c0:c0 + 64 * H].rearrange("p (h d) -> p h d", h=H)
        xr = x[:, :, 0:16]
        C = rope[0:nq, tix, 0:16].unsqueeze(1).broadcast_to([nq, H, 16])
        Sn = rope[0:nq, tix, 16:32].unsqueeze(1).broadcast_to([nq, H, 16])
        t1 = ring(smr, smc)
        t2 = ring(smr, smc)
        t1v = t1[0:nq, 0:16 * H].rearrange("p (h d) -> p h d", h=H)
        t2v = t2[0:nq, 0:16 * H].rearrange("p (h d) -> p h d", h=H)
        S.op("dve", lambda e: e.tensor_tensor(out=t1v, in0=xr, in1=C, op=ALU.mult), r=[buf, rope], w=[t1])
        yield
        S.op("dve", lambda e: e.tensor_tensor(out=t2v[:, :, 0:8], in0=x[:, :, 8:16], in1=Sn[:, :, 0:8], op=ALU.mult), r=[buf, rope], w=[t2])
        yield
        S.op("dve", lambda e: e.tensor_tensor(out=t2v[:, :, 8:16], in0=x[:, :, 0:8], in1=Sn[:, :, 8:16], op=ALU.mult), r=[buf, rope, t2], w=[t2])
        yield
        S.op("dve", lambda e: e.tensor_tensor(out=xr, in0=t1v, in1=t2v, op=ALU.add), r=[t1, t2], w=[buf])
        yield

    def ln_g(src_ap, dst_ap, nq, Dn, g_ap, b_ap, r, w, ptiles):
        nch = (Dn + 511) // 512
        st = ring(smr, smc)
        for c in range(nch):
            cw = min(512, Dn - c * 512)
            S.op("dve", lambda e, c=c, cw=cw: e.bn_stats(out=st[0:nq, 6 * c:6 * c + 6], in_=src_ap[:, c * 512:c * 512 + cw]), r=r, w=[st])
            yield
        mv = ring(smr, smc)
        S.op("dve", lambda e: e.bn_aggr(out=mv[0:nq, 0:2], in_=st[0:nq, 0:6 * nch]), r=[st], w=[mv])
        yield
        S.op("dve", lambda e: e.tensor_scalar(out=mv[0:nq, 2:3], in0=mv[0:nq, 1:2], scalar1=EPS, scalar2=None, op0=ALU.add), r=[mv], w=[mv])
        yield
        S.op("act", lambda e: e.activation(out=mv[0:nq, 4:5], in_=mv[0:nq, 2:3], func=AF.Ln), r=[mv], w=[mv])
        yield
        S.op("act", lambda e: e.activation(out=mv[0:nq, 3:4], in_=mv[0:nq, 4:5], func=AF.Exp, scale=-0.5), r=[mv], w=[mv])
        yield
        S.op("dve", lambda e: e.tensor_scalar(out=mv[0:nq, 5:6], in0=mv[0:nq, 0:1], scalar1=mv[0:nq, 3:4], scalar2=-1.0, op0=ALU.mult, op1=ALU.mult), r=[mv], w=[mv])
        yield
        S.op("act", lambda e: e.activation(out=dst_ap, in_=src_ap, func=AF.Identity, scale=mv[0:nq, 3:4], bias=mv[0:nq, 5:6]), r=r + [mv], w=w)
        yield
        S.op("dve", lambda e: e.tensor_tensor(out=dst_ap, in0=dst_ap, in1=g_ap, op=ALU.mult), r=w + ptiles, w=w)
        yield
        S.op("dve", lambda e: e.tensor_tensor(out=dst_ap, in0=dst_ap, in1=b_ap, op=ALU.add), r=w + ptiles, w=w)
        yield

    def layer_pass(tiles, l, last, xT_ready=False):
        NT = len(tiles)
        TT = sum(t["nq"] for t in tiles)
        S.dma("sp", p_ln[:, 0, :], ln1g[l:l + 1, :].partition_broadcast(128), lsem["ln"], w=[p_ln])
        S.dma("sp", p_ln[:, 1, :], ln1b[l:l + 1, :].partition_broadcast(128), lsem["ln"], w=[p_ln])
        S.dma("sp", p_sgu[:, 0, :], sgu_g[l:l + 1, :].partition_broadcast(128), lsem["sgu"], w=[p_sgu])
        S.dma("sp", p_sgu[:, 1, :], sgu_b[l:l + 1, :].partition_broadcast(128), lsem["sgu"], w=[p_sgu])
        S.dma("sp", p_idx[:, 0, :], idx_g[l:l + 1, :].partition_broadcast(128), lsem["idx"], w=[p_idx])
        S.dma("sp", p_idx[:, 1, :], idx_b[l:l + 1, :].partition_broadcast(128), lsem["idx"], w=[p_idx])

        def make_xT_g(ti, t, pool="a"):
            nq, col = t["nq"], t["col"]
            xb = ring(b16r, b16c)
            S.op("act", lambda e: e.activation(out=xb[0:nq, :], in_=xres[0:nq, ti, :], func=AF.Copy), r=[xrt[ti]], w=[xb])
            yield
            yield from transposes_g(xb, lambda j: xb[0:nq, j * 128:(j + 1) * 128], 8, nq,
                                    lambda e, bv, j, nj: e.tensor_copy(out=actT[:, j:j + nj, col:col + nq], in_=bv[:, 0:nj * 128].rearrange("p (j q) -> p j q", j=nj)[:, :, 0:nq]),
                                    [actT], pool=pool)

        def ln_xT_g(ti, t, do_xT):
            nq = t["nq"]
            yield from ln_g(xres[0:nq, ti, :], xres[0:nq, ti, :], nq, D, p_ln[0:nq, 0, :], p_ln[0:nq, 1, :], [xrt[ti]], [xrt[ti]], [p_ln])
            if do_xT:
                yield from make_xT_g(ti, t, pool="all")

        if not xT_ready:
            for p0 in range(0, NT, 2):
                rr([make_xT_g(ti, tiles[ti]) for ti in range(p0, min(p0 + 2, NT))])

        stage(2)
        blocks = [(0, 512), (512, 512), (1024, 68), (1092, 512), (1604, 512), (2116, 512), (2628, 512), (3140, 512), (3652, 512)]

        def mm_block(bi, pool, tis=None, wc=None):
            c0, ncols = blocks[bi]
            if wc is None:
                wc = [wchunk("w_in", l, r, c0, ncols) for r in range(2)]
            accs = {}
            for ti in (range(NT) if tis is None else tis):
                t = tiles[ti]
                nq, col = t["nq"], t["col"]
                acc = bank(pool)
                for kc in range(8):
                    S.op("pe", lambda e, kc=kc: e.matmul(acc[0:nq, 0:ncols], lhsT=actT[:, kc, col:col + nq], rhs=wc[kc // 4][:, kc % 4, 0:ncols], start=(kc == 0), stop=(kc == 7)),
                         r=[actT, wc[kc // 4]], w=[acc], signal=(kc == 7 or kc == 3))
                accs[ti] = acc
                if bi == 3:
                    S.op("act", lambda e: e.activation(out=ub[0:nq, ti, :], in_=acc[0:nq, :], func=AF.Copy), r=[acc], w=[ub])
                elif bi >= 5:
                    gc = (bi - 5) * 512
                    S.op("act", lambda e: e.activation(out=gates[0:nq, ti, gc:gc + 512], in_=acc[0:nq, :], func=AF.Sigmoid), r=[acc], w=[gates])
            return accs, wc

        def chain(bi, ti, t, acc):
            nq, col, tix = t["nq"], t["col"], t["tix"]
            kT_, V_, KiT_ = t["cache"][l]
            kpos = t["kpos"]
            if bi == 0:
                qf = ring(f32r, f32c)
                S.op("act", lambda e: e.activation(out=qf[0:nq, :], in_=acc[0:nq, :], func=AF.Copy, scale=0.125), r=[acc], w=[qf])
                yield
                yield from rope_g(qf, 0, 8, nq, tix)
                qb = ring(b16r, b16c)
                S.op("dve", lambda e: e.tensor_copy(out=qb[0:nq, 0:512].rearrange("p (g n d) -> p g n d", g=4, n=2),
                                                    in_=qf[0:nq, :].rearrange("p (n g d) -> p g n d", n=2, g=4)), r=[qf], w=[qb])
                yield
                yield from transposes_g(qb, lambda j: qb[0:nq, j * 128:(j + 1) * 128], 4, nq,
                                        lambda e, bv, j, nj: e.tensor_copy(out=QT[:, ti, 0:4 * nq].rearrange("p (g q) -> p g q", g=4),
                                                                           in_=bv[:, 0:512].rearrange("p (g q) -> p g q", g=4)[:, :, 0:nq]), [QT])
            elif bi == 1:
                kf = ring(kout, oc)
                ki_ = (oc[0] - 1) % 2
                vf = vout[ki_]
                S.op("act", lambda e: e.activation(out=kf[0:nq, :], in_=acc[0:nq, 0:128], func=AF.Copy), r=[acc], w=[kf])
                yield
                S.op("act", lambda e: e.activation(out=vf[0:nq, :], in_=acc[0:nq, 128:256], func=AF.Copy), r=[acc], w=[vf])
                yield
                qif = ring(f32r, f32c)
                S.op("act", lambda e: e.activation(out=qif[0:nq, 0:256], in_=acc[0:nq, 256:512], func=AF.Copy), r=[acc], w=[qif])
                yield
                S.dma("sp", t["outs"]["v"][l], vf[0:nq, :], osem[f"v{ki_}"], r=[vf])
                S.op("dve", lambda e: e.tensor_copy(out=V_[0:nq, kpos // 128, :, 0:64], in_=vf[0:nq, :].rearrange("p (n d) -> p n d", n=2)), r=[vf], w=[V_])
                yield
                yield from rope_g(kf, 0, 2, nq, tix)
                S.dma("sp", t["outs"]["k"][l], kf[0:nq, :], osem[f"k{ki_}"], r=[kf])
                kb = ring(b16r, b16c)
                S.op("dve", lambda e: e.tensor_copy(out=kb[0:nq, 0:128], in_=kf[0:nq, :]), r=[kf], w=[kb])
                yield
                yield from transposes_g(kb, lambda j: kb[0:nq, 0:128], 1, nq,
                                        lambda e, bv, j, nj: e.tensor_copy(out=kT_[:, kpos:kpos + nq], in_=bv[:, 0:nq]), [kT_])
                yield from rope_g(qif, 0, 4, nq, tix)
                qib = ring(b16r, b16c)
                S.op("dve", lambda e: e.tensor_copy(out=qib[0:nq, 0:256], in_=qif[0:nq, 0:256]), r=[qif], w=[qib])
                yield
                yield from transposes_g(qib, lambda j: qib[0:nq, j * 128:(j + 1) * 128], 2, nq,
                                        lambda e, bv, j, nj: e.tensor_copy(out=QiT[:, ti, :, 0:nq], in_=bv[:, 0:256].rearrange("p (j q) -> p j q", j=2)[:, :, 0:nq]), [QiT])
            elif bi == 2:
                S.op("act", lambda e: e.activation(out=absw[0:nq, ti, :], in_=acc[0:nq, 64:68], func=AF.Abs, scale=0.0625), r=[acc], w=[absw])
                yield
                S.op("act", lambda e: e.activation(out=sgnw[0:nq, ti, :], in_=acc[0:nq, 64:68], func=AF.Sign), r=[acc], w=[sgnw])
                yield
                kif = ring(kiout, oc2)
                kii = (oc2[0] - 1) % 2
                yield from ln_g(acc[0:nq, 0:64], kif[0:nq, :], nq, 64, p_idx[0:nq, 0, :], p_idx[0:nq, 1, :], [acc], [kif], [p_idx])
                yield from rope_g(kif, 0, 1, nq, tix)
                S.dma("sp", t["outs"]["ki"][l], kif[0:nq, :], osem[f"ki{kii}"], r=[kif])
                kib = ring(b16r, b16c)
                S.op("dve", lambda e: e.tensor_copy(out=kib[0:nq, 0:128].rearrange("p (r d) -> p r d", r=2), in_=kif[0:nq, :].unsqueeze(1).broadcast_to([nq, 2, 64])), r=[kif], w=[kib])
                yield
                yield from transposes_g(kib, lambda j: kib[0:nq, 0:128], 1, nq,
                                        lambda e, bv, j, nj: e.tensor_copy(out=KiT_[:, kpos:kpos + nq], in_=bv[:, 0:nq]), [KiT_])
            elif bi == 4:
                vn = ring(f32r, f32c)
                yield from ln_g(acc[0:nq, :], vn[0:nq, :], nq, 512, p_sgu[0:nq, 0, :], p_sgu[0:nq, 1, :], [acc], [vn], [p_sgu])
                if t["outs"].get("vn") is not None:
                    S.dma("sp", t["outs"]["vn"][l], vn[0:nq, :], vnsem[f32r.index(vn)], r=[vn])
                vnb = ring(b16r, b16c)
                S.op("act", lambda e: e.activation(out=vnb[0:nq, 0:512], in_=vn[0:nq, :], func=AF.Copy), r=[vn], w=[vnb])
                yield
                mix = bank("a")
                for g in range(4):
                    S.op("pe", lambda e, g=g: e.matmul(mix[0:nq, g * 128:(g + 1) * 128], lhsT=WsT[l][0:nq, g, 0:nq], rhs=vnb[0:nq, g * 128:(g + 1) * 128], start=True, stop=True),
                         r=[WsT[l], vnb], w=[mix], signal=(g == 3))
                yield
                ob = ring(b16r, b16c)
                for g in range(4):
                    S.op("dve", lambda e, g=g: e.scalar_tensor_tensor(out=ob[0:nq, g * 128:(g + 1) * 128], in0=mix[0:nq, g * 128:(g + 1) * 128], scalar=bsT[l][0:nq, g:g + 1],
                                                                      in1=ub[0:nq, ti, g * 128:(g + 1) * 128], op0=ALU.add, op1=ALU.mult), r=[mix, bsT[l], ub], w=[ob])
                    yield
                yield from transposes_g(ob, lambda j: ob[0:nq, j * 128:(j + 1) * 128], 4, nq,
                                        lambda e, bv, j, nj: e.tensor_copy(out=ObT[:, 0:4, col:col + nq], in_=bv[:, 0:512].rearrange("p (j q) -> p j q", j=4)[:, :, 0:nq]), [ObT])

        halves = [list(range(0, min(2, NT))), list(range(2, NT))]
        halves = [h for h in halves if h]
        cseq, sseq = [0, 1, 2, 4], [5, 6, 7, 8]
        mm_block(3, "s")
        cur = {}
        wc0 = None
        for h in halves:
            a_, wc0 = mm_block(cseq[0], "ip", h, wc0)
            cur.update(a_)
        for i, cbi in enumerate(cseq):
            stage(2 + (cbi + 1) / 10.0)
            mm_block(sseq[i], "s")
            nxt = {}
            wcn = None
            for h in halves:
                rr([chain(cbi, ti, tiles[ti], cur[ti]) for ti in h])
                if i + 1 < len(cseq):
                    a_, wcn = mm_block(cseq[i + 1], "ip", h, wcn)
                    nxt.update(a_)
            cur = nxt

        stage(3)
        Ob = [banks[6], banks[7]]

        def SC(ti):
            t = tiles[ti]
            nq = t["nq"]
            KiT_ = t["cache"][l][2]
            Sk = t["nkeys"]
            for c0 in range(0, Sk, 512):
                cw = min(512, Sk - c0)
                for h in range(4):
                    hp, hh = h // 2, h % 2
                    dk = bank("a")
                    S.op("pe", lambda e: e.matmul(dk[0:nq, 0:cw], lhsT=QiT[64 * hh:64 * hh + 64, ti, hp, 0:nq], rhs=KiT_[64 * hh:64 * hh + 64, c0:c0 + cw], start=True, stop=True),
                         r=[QiT, KiT_], w=[dk])
                    rl = ring(f32r, f32c)
                    S.op("act", lambda e: e.activation(out=rl[0:nq, 0:cw], in_=dk[0:nq, 0:cw], func=AF.Relu, scale=absw[0:nq, ti, h:h + 1]), r=[dk, absw], w=[rl])
                    if h == 0:
                        S.op("dve", lambda e: e.tensor_scalar(out=scores[0:nq, c0:c0 + cw], in0=rl[0:nq, 0:cw], scalar1=sgnw[0:nq, ti, 0:1], scalar2=None, op0=ALU.mult), r=[rl, sgnw], w=[scores])
                    else:
                        S.op("dve", lambda e: e.scalar_tensor_tensor(out=scores[0:nq, c0:c0 + cw], in0=rl[0:nq, 0:cw], scalar=sgnw[0:nq, ti, h:h + 1], in1=scores[0:nq, c0:c0 + cw], op0=ALU.mult, op1=ALU.add),
                             r=[rl, sgnw, scores], w=[scores])
                S.op("dve", lambda e: e.tensor_reduce(out=bmax[0:nq, c0 // 512:c0 // 512 + 1], in_=scores[0:nq, c0:c0 + cw], axis=AX.X, op=ALU.max, apply_absolute_value=True), r=[scores], w=[bmax])

        def BI(ti):
            t = tiles[ti]
            nq = t["nq"]
            Sk = t["nkeys"]
            K = t["topk"]
            S1 = Sk
            if SPLIT and Sk >= 512:
                S1 = int(round(SPLIT * Sk / 64.0)) * 64
            N2 = Sk - S1
            nblk_s = (Sk + 511) // 512
            S.op("dve", lambda e: e.tensor_reduce(out=cntt[0:nq, 1:2], in_=bmax[0:nq, 0:nblk_s], axis=AX.X, op=ALU.max), r=[bmax], w=[cntt])
            S.op("dve", lambda e: e.tensor_scalar(out=cntt[0:nq, 1:2], in0=cntt[0:nq, 1:2], scalar1=1e-20, scalar2=None, op0=ALU.add), r=[cntt], w=[cntt])
            if t["causal"]:
                S.op("dve", lambda e: e.memset(scores[0:64, Sk - 64:Sk], NEG), r=[cntt], w=[scores])
            S.op("dve", lambda e: e.tensor_scalar(out=stp[0:nq, :], in0=steps[0:nq, :], scalar1=cntt[0:nq, 1:2], scalar2=None, op0=ALU.mult), r=[steps, cntt], w=[stp])
            S.op("dve", lambda e: e.tensor_scalar(out=tcur[0:nq, 0:1], in0=cntt[0:nq, 1:2], scalar1=0.37, scalar2=None, op0=ALU.mult), r=[cntt], w=[tcur])
            yield "S"
            for it in range(NIT + 1):
                S.op("dve", lambda e: e.tensor_scalar(out=maskb[0:nq, 0:S1], in0=scores[0:nq, 0:S1], scalar1=tcur[0:nq, 0:1], scalar2=0.0, op0=ALU.is_ge, op1=ALU.add, accum_out=cntt[0:nq, 0:1]),
                     r=[scores, tcur], w=[maskb, cntt])
                if N2:
                    S.op("act", lambda e: e.activation(out=maskb[0:nq, S1:Sk], in_=scores[0:nq, S1:Sk], func=AF.Sign, scale=-1.0, bias=tcur[0:nq, 0:1], accum_out=cnta[0:nq, 0:1]),
                         r=[scores, tcur], w=[maskb_hi, cnta])
                yield "A"
                if N2:
                    S.op("dve", lambda e: e.scalar_tensor_tensor(out=dcur[0:nq, 1:2], in0=cntt[0:nq, 0:1], scalar=2.0, in1=cnta[0:nq, 0:1], op0=ALU.mult, op1=ALU.subtract), r=[cntt, cnta], w=[dcur])
                    zsrc, thr = dcur[0:nq, 1:2], 2 * K - N2 - 0.5
                else:
                    zsrc, thr = cntt[0:nq, 0:1], K - 0.5
                last = (it == NIT)
                S.op("dve", lambda e: e.tensor_scalar(out=dcur[0:nq, 0:1], in0=zsrc, scalar1=thr, scalar2=(1.0 if last else 0.5), op0=ALU.is_ge, op1=ALU.subtract), r=[cntt, dcur], w=[dcur])
                S.op("dve", lambda e: e.scalar_tensor_tensor(out=tcur[0:nq, 0:1], in0=dcur[0:nq, 0:1], scalar=stp[0:nq, it + 1:it + 2], in1=tcur[0:nq, 0:1], op0=ALU.mult, op1=ALU.add), r=[tcur, dcur, stp], w=[tcur])
                yield "B"
            S.op("dve", lambda e: e.tensor_scalar(out=maskb[0:nq, 0:Sk], in0=scores[0:nq, 0:Sk], scalar1=tcur[0:nq, 0:1], scalar2=None, op0=ALU.is_ge), r=[scores, tcur], w=[maskb, maskb_hi])

        def kblocks_of(ti):
            Sk = tiles[ti]["nkeys"]
            return [(k0, min(128, Sk - k0)) for k0 in range(0, Sk, 128)]

        def MT(ti):
            t = tiles[ti]
            nq = t["nq"]
            kblocks = kblocks_of(ti)
            nkb = len(kblocks)
            j = 0
            while j < nkb:
                nj = min(8, nkb - j)
                bk = bank("a")
                for jj in range(nj):
                    k0, ks = kblocks[j + jj]
                    S.op("pe", lambda e: e.transpose(out=bf(bk)[0:ks, jj * 128:jj * 128 + nq], in_=maskb[0:nq, k0:k0 + ks], identity=ident[0:nq, 0:nq]),
                         r=[maskb, ident], w=[bk], signal=(jj == nj - 1))
                ksz = kblocks[j + nj - 1][1]
                nfull = nj if ksz == 128 else nj - 1
                if nfull > 0:
                    S.op("act", lambda e: e.activation(out=maskT[:, j * 128:(j + nfull) * 128].rearrange("p (b q) -> p b q", b=nfull)[:, :, 0:nq],
                                                       in_=bf(bk)[:, 0:nfull * 128].rearrange("p (b q) -> p b q", b=nfull)[:, :, 0:nq], func=AF.Copy), r=[bk], w=[maskT])
                if ksz != 128:
                    S.op("act", lambda e: e.activation(out=maskT[0:ksz, (j + nj - 1) * 128:(j + nj - 1) * 128 + nq], in_=bf(bk)[0:ksz, (nj - 1) * 128:(nj - 1) * 128 + nq], func=AF.Copy), r=[bk], w=[maskT])
                j += nj

        def AT(ti, per):
            t = tiles[ti]
            nq = t["nq"]
            kT_, V_, _ = t["cache"][l]
            kblocks = kblocks_of(ti)
            nkb = len(kblocks)
            for n in range(2):
                S.op("dve", lambda e: e.memset(Ob[n][:, :], 0.0), w=[Ob[n]])
            its = [(kb_i, k0, ks, n) for kb_i, (k0, ks) in enumerate(kblocks) for n in range(2)]

            def emit_lg(kb_i, k0, ks, n):
                lg = bank("b")
                S.op("pe", lambda e: e.matmul(lg[0:ks, 0:4 * nq], lhsT=kT_[64 * n:64 * n + 64, k0:k0 + ks], rhs=QT[64 * n:64 * n + 64, ti, 0:4 * nq], start=True, stop=True),
                     r=[kT_, QT], w=[lg])
                return lg

            def emit_E(x, lg):
                kb_i, k0, ks, n = x
                pt = ring(PTr, ptc)
                S.op("act", lambda e: e.activation(out=pt[0:ks, 0:4 * nq], in_=lg[0:ks, 0:4 * nq], func=AF.Exp), r=[lg], w=[pt])
                return pt

            def emit_M(x, pt):
                kb_i, k0, ks, n = x
                S.op(MULT_ENG, lambda e: e.tensor_tensor(out=pt[0:ks, 0:4 * nq].rearrange("p (g q) -> p g q", g=4), in0=pt[0:ks, 0:4 * nq].rearrange("p (g q) -> p g q", g=4),
                                                         in1=maskT[0:ks, kb_i * 128:kb_i * 128 + nq].unsqueeze(1).broadcast_to([ks, 4, nq]), op=ALU.mult), r=[pt, maskT], w=[pt])
                for g in range(4):
                    S.op("pe", lambda e, g=g: e.matmul(Ob[n][0:nq, g * 65:(g + 1) * 65], lhsT=pt[0:ks, g * nq:(g + 1) * nq], rhs=V_[0:ks, kb_i, n, :], start=False, stop=(kb_i == nkb - 1), skip_group_check=True),
                         r=[pt, V_], w=[Ob[n]], signal=(g == 3))

            chunks = [its[i:i + per] for i in range(0, len(its), per)]
            look = max(1, 6 // per - 1)
            lgq = {}
            for i in range(min(look, len(chunks))):
                lgq[i] = [emit_lg(*x) for x in chunks[i]]
            for ci, ch in enumerate(chunks):
                if ci + look < len(chunks):
                    lgq[ci + look] = [emit_lg(*x) for x in chunks[ci + look]]
                pts = [emit_E(x, lg) for x, lg in zip(ch, lgq.pop(ci))]
                yield "E"
                for x, pt in zip(ch, pts):
                    emit_M(x, pt)
                yield "M"

        def FIN(ti):
            t = tiles[ti]
            nq, col = t["nq"], t["col"]
            rd = ring(smr, smc)
            on = ring(b16r, b16c)
            for n in range(2):
                ov = Ob[n][0:nq, 0:260].rearrange("p (g c) -> p g c", g=4)
                S.op("dve", lambda e: e.reciprocal(out=rd[0:nq, 4 * n:4 * n + 4], in_=ov[:, :, 64]), r=[Ob[n]], w=[rd])
                S.op("dve", lambda e: e.tensor_tensor(out=on[0:nq, 256 * n:256 * n + 256].rearrange("p (g d) -> p g d", g=4), in0=ov[:, :, 0:64],
                                                      in1=rd[0:nq, 4 * n:4 * n + 4].unsqueeze(2).broadcast_to([nq, 4, 64]), op=ALU.mult), r=[Ob[n], rd], w=[on])
            transposes(on, lambda j: on[0:nq, j * 128:(j + 1) * 128], 4, nq,
                       lambda e, bv, j, nj: e.tensor_copy(out=OaT[:, 0:4, col:col + nq], in_=bv[:, 0:512].rearrange("p (j q) -> p j q", j=4)[:, :, 0:nq]), [OaT])

        def per_of(ti):
            n_at = 2 * len(kblocks_of(ti))
            return max(1, min(3, -(-n_at // (NIT + 1))))

        prev = None
        for ti in range(NT):
            SC(ti)
            at = AT(prev, per_of(prev)) if prev is not None else None
            if at is not None:
                next(at, None)
            for _tag in BI(ti):
                if at is not None:
                    next(at, None)
            if at is not None:
                for _ in at:
                    pass
                FIN(prev)
            MT(ti)
            prev = ti
        for _ in AT(prev, 2):
            pass
        FIN(prev)

        stage(4)
        wpa = [wchunk("w_pa", l, 0, cb * 512, 512) for cb in range(2)]
        wpb = [wchunk("w_pb", l, 0, cb * 512, 512) for cb in range(2)]
        for ti, t in enumerate(tiles):
            nq, col = t["nq"], t["col"]
            mg = ring(b16r, b16c)
            for cb in range(2):
                pa = bank("all")
                for kc in range(4):
                    S.op("pe", lambda e, pa=pa, kc=kc, cb=cb: e.matmul(pa[0:nq, :], lhsT=OaT[:, kc, col:col + nq], rhs=wpa[cb][:, kc, :], start=(kc == 0), stop=(kc == 3)), r=[OaT, wpa[cb]], w=[pa], signal=(kc == 3))
                pb = bank("all")
                for kc in range(4):
                    S.op("pe", lambda e, pb=pb, kc=kc, cb=cb: e.matmul(pb[0:nq, :], lhsT=ObT[:, kc, col:col + nq], rhs=wpb[cb][:, kc, :], start=(kc == 0), stop=(kc == 3)), r=[ObT, wpb[cb]], w=[pb], signal=(kc == 3))
                m1 = ring(f32r, f32c)
                m2 = ring(f32r, f32c)
                S.op("dve", lambda e, pa=pa, m1=m1, cb=cb: e.tensor_tensor(out=m1[0:nq, :], in0=pa[0:nq, :], in1=gates[0:nq, ti, cb * 512:(cb + 1) * 512], op=ALU.mult), r=[pa, gates], w=[m1])
                S.op("dve", lambda e, pb=pb, m2=m2, cb=cb: e.tensor_tensor(out=m2[0:nq, :], in0=pb[0:nq, :], in1=gates[0:nq, ti, 1024 + cb * 512:1024 + (cb + 1) * 512], op=ALU.mult), r=[pb, gates], w=[m2])
                S.op("pool", lambda e, m1=m1, m2=m2, mg=mg, cb=cb: e.tensor_tensor(out=mg[0:nq, cb * 512:(cb + 1) * 512], in0=m1[0:nq, :], in1=m2[0:nq, :], op=ALU.add), r=[m1, m2], w=[mg])
            transposes(mg, lambda j, mg=mg: mg[0:nq, j * 128:(j + 1) * 128], 8, nq,
                       lambda e, bv, j, nj, col=col, nq=nq: e.tensor_copy(out=actT[:, j:j + nj, col:col + nq], in_=bv[:, 0:nj * 128].rearrange("p (j q) -> p j q", j=nj)[:, :, 0:nq]), [actT])
        for cb in range(2):
            wo = [wchunk("w_out", l, r, cb * 512, 512) for r in range(2)]
            for ti, t in enumerate(tiles):
                nq, col = t["nq"], t["col"]
                acc = bank("all")
                for kc in range(8):
                    S.op("pe", lambda e, acc=acc, kc=kc: e.matmul(acc[0:nq, :], lhsT=actT[:, kc, col:col + nq], rhs=wo[kc // 4][:, kc % 4, :], start=(kc == 0), stop=(kc == 7)), r=[actT, wo[kc // 4]], w=[acc], signal=(kc == 7 or kc == 3))
                S.op("dve", lambda e, acc=acc, ti=ti, cb=cb, nq=nq: e.scalar_tensor_tensor(out=xres[0:nq, ti, cb * 512:(cb + 1) * 512], in0=xres[0:nq, ti, cb * 512:(cb + 1) * 512], scalar=ALPHA, in1=acc[0:nq, :], op0=ALU.mult, op1=ALU.add), r=[acc, xrt[ti]], w=[xrt[ti]])
        rr([ln_xT_g(ti, t, True) for ti, t in enumerate(tiles)])
        S.dma("sp", p_ln[:, 0, :], ln2g[l:l + 1, :].partition_broadcast(128), lsem["ln"], w=[p_ln])
        S.dma("sp", p_ln[:, 1, :], ln2b[l:l + 1, :].partition_broadcast(128), lsem["ln"], w=[p_ln])
        for fb in range(8):
            w1 = [wchunk("w_ff1", l, r, fb * 512, 512) for r in range(2)]
            for fc in range(4):
                acc = bank("all")
                for kc in range(8):
                    S.op("pe", lambda e, acc=acc, kc=kc, fc=fc: e.matmul(acc[:, 0:TT], lhsT=w1[kc // 4][:, kc % 4, fc * 128:(fc + 1) * 128], rhs=actT[:, kc, 0:TT], start=(kc == 0), stop=(kc == 7)), r=[actT, w1[kc // 4]], w=[acc], signal=(kc == 7 or kc == 3))
                rl = ring(f32r, f32c)
                S.op("act", lambda e, acc=acc, rl=rl: e.activation(out=rl[:, 0:TT], in_=acc[:, 0:TT], func=AF.Relu), r=[acc], w=[rl])
                S.op("dve", lambda e, rl=rl, fb=fb, fc=fc: e.tensor_tensor(out=HT[:, (fb * 4 + fc) * 512:(fb * 4 + fc) * 512 + TT], in0=rl[:, 0:TT], in1=rl[:, 0:TT], op=ALU.mult), r=[rl], w=[HT])
        for cb in range(2):
            accs = [banks[4 + ti] for ti in range(NT)]
            for r in range(8):
                w2 = wchunk("w_ff2", l, r, cb * 512, 512)
                for ti, t in enumerate(tiles):
                    nq, col = t["nq"], t["col"]
                    for k4 in range(4):
                        S.op("pe", lambda e, ti=ti, k4=k4, r=r, nq=nq, col=col, w2=w2: e.matmul(accs[ti][0:nq, :], lhsT=HT[:, (4 * r + k4) * 512 + col:(4 * r + k4) * 512 + col + nq], rhs=w2[:, k4, :], start=(r == 0 and k4 == 0), stop=(r == 7 and k4 == 3)),
                             r=[HT, w2], w=[accs[ti]], signal=(k4 == 3))
            for ti, t in enumerate(tiles):
                nq = t["nq"]
                S.op("dve", lambda e, ti=ti, cb=cb, nq=nq: e.scalar_tensor_tensor(out=xres[0:nq, ti, cb * 512:(cb + 1) * 512], in0=xres[0:nq, ti, cb * 512:(cb + 1) * 512], scalar=ALPHA, in1=accs[ti][0:nq, :], op0=ALU.mult, op1=ALU.add), r=[accs[ti], xrt[ti]], w=[xrt[ti]])
        rr([ln_xT_g(ti, t, not last) for ti, t in enumerate(tiles)])
        if last:
            for ti, t in enumerate(tiles):
                S.dma("sp", t["outs"]["y"], xres[0:t["nq"], ti, :], osem["y"], r=[xrt[ti]])
        return not last

    pcache = {l: (KT[l], Vc[l], KiT[l]) for l in range(NL)}
    try:
      stage(1)
      for j in range(NST):
          tok0 = j * 512
          S.dma("sp", xres[:, :, :], xp[tok0:tok0 + 512, :].rearrange("(t p) d -> p t d", p=128), xsem, w=[xres])
          tiles = []
          for t in range(4):
              g = j * 4 + t
              p0 = g * 128
              tiles.append(dict(topk=min(TOPK, SEQ // 4), nq=128, col=t * 128, tix=g, cache=pcache, kpos=p0, nkeys=p0 + 128, causal=True,
                                outs=dict(k=[pk[l, p0:p0 + 128, :] for l in range(NL)], v=[pv[l, p0:p0 + 128, :] for l in range(NL)],
                                          ki=[pik[l, p0:p0 + 128, :] for l in range(NL)], y=yp[p0:p0 + 128, :])))
          ready = False
          for l in range(NL):
              ready = layer_pass(tiles, l, l == NL - 1, ready)

      S.dma("sp", xres[0:32, 0:2, :], xs.rearrange("(t p) d -> p t d", p=32), xsem, w=[xres])
      ready_s = False
      for l in range(NL):
          scache = {}
          for s in range(2):
              kT_, V_, KiT_ = KT[s], Vc[s], KiT[s]
              for (src, width, kind) in ((ck, 128, "k"), (cv, 128, "v"), (cik, 64, "ki")):
                  S.dma("sp", scores[:, 0:PBLK * width].rearrange("p (b c) -> p b c", b=PBLK), src[l, s].rearrange("(b p) c -> p b c", p=128), stsem, w=[scores])
                  if kind == "v":
                      S.op("dve", lambda e, V_=V_: e.tensor_copy(out=V_[:, 0:PBLK, :, 0:64], in_=scores[:, 0:PBLK * 128].rearrange("p (b n d) -> p b n d", b=PBLK, n=2)), r=[scores], w=[V_])
                      continue
                  for b0 in range(0, PBLK, 8):
                      nb = min(8, PBLK - b0)
                      cb16 = ring(b16r, b16c)
                      if kind == "k":
                          S.op("dve", lambda e, cb16=cb16, b0=b0, nb=nb: e.tensor_copy(out=cb16[:, 0:nb * 128], in_=scores[:, b0 * 128:(b0 + nb) * 128]), r=[scores], w=[cb16])
                          dstT = kT_
                      else:
                          S.op("dve", lambda e, cb16=cb16, b0=b0, nb=nb: e.tensor_copy(out=cb16[:, 0:nb * 128].rearrange("p (b r d) -> p b r d", b=nb, r=2),
                                                                               in_=scores[:, b0 * 64:(b0 + nb) * 64].rearrange("p (b d) -> p b d", b=nb).unsqueeze(2).broadcast_to([128, nb, 2, 64])), r=[scores], w=[cb16])
                          dstT = KiT_
                      transposes(cb16, lambda j, cb16=cb16: cb16[:, j * 128:(j + 1) * 128], nb, 128,
                                 lambda e, bv, j, nj, dstT=dstT, b0=b0: e.tensor_copy(out=dstT[:, (b0 + j) * 128:(b0 + j + nj) * 128], in_=bv[:, 0:nj * 128]), [dstT])
              scache[s] = {l: (kT_, V_, KiT_)}
          tiles = []
          for s in range(2):
              tiles.append(dict(topk=min(TOPK, (PAST + 32) // 4), nq=32, col=s * 32, tix=NTT, cache=scache[s], kpos=PAST, nkeys=PAST + 32, causal=False,
                                outs=dict(k=[sk[ll, 32 * s:32 * s + 32, :] for ll in range(NL)], v=[sv[ll, 32 * s:32 * s + 32, :] for ll in range(NL)],
                                          ki=[sik[ll, 32 * s:32 * s + 32, :] for ll in range(NL)], vn=[ssv[ll, 32 * s:32 * s + 32, :] for ll in range(NL)],
                                          y=ys[32 * s:32 * s + 32, :])))
          ready_s = layer_pass(tiles, l, l == NL - 1, ready_s)
    except _Stop:
        pass

    S.final_waits("sp")
    print("instructions per engine:", S.ninst, "signals:", {k: v.n for k, v in S.E.items()}, flush=True)
    es.close()
    return nc


def _consts(SEQ, PAST):
    NTT = SEQ // 128
    half = 8
    freqs = (500000.0 ** (-np.arange(half, dtype=np.float32) * 2.0 / 16)).astype(np.float32)
    rope = np.zeros((128, NTT + 1, 32), np.float32)
    for ti in range(NTT + 1):
        pos = (np.arange(128) + (ti * 128 if ti < NTT else PAST)).astype(np.float32)
        ang = (pos[:, None] * freqs[None, :]).astype(np.float32)
        c, s = np.cos(ang).astype(np.float32), np.sin(ang).astype(np.float32)
        rope[:, ti, 0:8] = c
        rope[:, ti, 8:16] = c
        rope[:, ti, 16:24] = -s
        rope[:, ti, 24:32] = s
    steps = np.zeros((128, NIT + 2), np.float32)
    for i in range(NIT + 1):
        steps[:, i] = 2.0 * 1.5 * 2.0 ** (-i)
    steps[:, NIT + 1] = 1.5 * 2.0 ** (-NIT)
    ident = np.eye(128, dtype=np.float32).astype(ml_dtypes.bfloat16)
    cm = (np.arange(128)[None, :] >= np.arange(128)[:, None]).astype(np.float32).astype(ml_dtypes.bfloat16)
    return dict(c_ident=ident, c_cmask=cm, c_rope=rope, c_steps=steps)


_NC_CACHE = {}


def run(inputs, SEQ, PAST, n_cores=8, stop=0):
    key = (SEQ, PAST, stop)
    if key not in _NC_CACHE:
        _NC_CACHE[key] = build(SEQ, PAST, stop)
    nc = _NC_CACHE[key]
    cs = _consts(SEQ, PAST)
    f = lambda a: np.ascontiguousarray(np.asarray(a, dtype=np.float32))
    wnames = ["w_in", "w_pa", "w_pb", "w_out", "w_ff1", "w_ff2", "idx_k_g", "idx_k_b", "sgu_ln_g", "sgu_ln_b", "w_s", "b_s",
              "ln1_g", "ln1_b", "ln2_g", "ln2_b"]
    shared = {k: f(inputs[k]) for k in wnames}
    xp = f(inputs["x_prompt"]); xs = f(inputs["x_sample"])
    ck = f(inputs["cache_k"]); cv = f(inputs["cache_v"]); cik = f(inputs["cache_idx_k"])
    in_maps = []
    for c in range(n_cores):
        m = dict(shared)
        m.update(cs)
        m["xp"] = xp[c]
        m["xs"] = xs[2 * c:2 * c + 2].reshape(64, D)
        m["ck"] = np.ascontiguousarray(ck[:, 2 * c:2 * c + 2].reshape(NL, 2, PAST, 128))
        m["cv"] = np.ascontiguousarray(cv[:, 2 * c:2 * c + 2].reshape(NL, 2, PAST, 128))
        m["cik"] = np.ascontiguousarray(cik[:, 2 * c:2 * c + 2])
        in_maps.append(m)
    res = run_bass_kernel_spmd(nc, in_maps, core_ids=list(range(n_cores)))
    R = res.results
    B = n_cores
    y_p = np.stack([R[c]["yp"] for c in range(B)], 0)
    y_s = np.concatenate([R[c]["ys"].reshape(2, 32, D) for c in range(B)], 0)
    p_k = np.stack([R[c]["pk"] for c in range(B)], 1).reshape(NL, B, SEQ, 2, 64)
    p_v = np.stack([R[c]["pv"] for c in range(B)], 1).reshape(NL, B, SEQ, 2, 64)
    p_ik = np.stack([R[c]["pik"] for c in range(B)], 1)
    s_k = np.concatenate([R[c]["sk"].reshape(NL, 2, 32, 2, 64) for c in range(B)], 1)
    s_v = np.concatenate([R[c]["sv"].reshape(NL, 2, 32, 2, 64) for c in range(B)], 1)
    s_ik = np.concatenate([R[c]["sik"].reshape(NL, 2, 32, 64) for c in range(B)], 1)
    s_sv = np.concatenate([R[c]["ssv"].reshape(NL, 2, 32, 512) for c in range(B)], 1)
    return tuple(np.asarray(a, dtype=np.float32) for a in (y_p, y_s, p_k, p_v, p_ik, s_k, s_v, s_ik, s_sv))


def kernel(**inputs):
    SEQ = int(np.asarray(inputs["x_prompt"]).shape[1])
    PAST = int(np.asarray(inputs["cache_k"]).shape[2])
    return run(inputs, SEQ, PAST, 8)
```
